# Optimizing a Trainium2 kernel written in Bass

```python
import math
import jax, jax.numpy as jnp
from jax import lax
import numpy as np

D_MODEL = 1024
BATCH = 4
SEQ = 4096
DEPTH = 1

N_META = 16
D_FF = 2816
D_CONV = 512
CONV_WIDTH = 31
D_SSM = 512
SSM_GROUP = 16
N_SSM_GROUPS = D_SSM // SSM_GROUP
SSM_STATE = 64
N_BRANCHES = 2
D_IN = 2 * D_CONV + D_SSM + N_BRANCHES * D_MODEL
DT_MIN = 1e-3
DT_MAX = 1e-1
EPS = 1e-6

kernel_name = "hybrid_meta_conformer_conv_s5_gated_macaron"


def rms_norm(x, g):
    xf = x.astype(jnp.float32)
    y = xf * lax.rsqrt(jnp.mean(xf * xf, axis=-1, keepdims=True) + EPS)
    return (y * g.astype(jnp.float32)).astype(x.dtype)


def swiglu_ffn(x, w1, w3, w2):
    return (jax.nn.silu(x @ w1) * (x @ w3)) @ w2


def conformer_conv_branch(a, dw, dw_b, ln_g, ln_b, w_proj):
    v, g = jnp.split(a, 2, axis=-1)
    z = v * jax.nn.sigmoid(g)
    z = lax.conv_general_dilated(
        z, dw[:, None, :].astype(z.dtype), window_strides=(1,),
        padding=((CONV_WIDTH - 1, 0),),
        dimension_numbers=("NWC", "WIO", "NWC"),
        feature_group_count=D_CONV) + dw_b
    zf = z.astype(jnp.float32)
    mu = jnp.mean(zf, axis=-1, keepdims=True)
    var = jnp.mean(jnp.square(zf - mu), axis=-1, keepdims=True)
    zf = (zf - mu) * lax.rsqrt(var + EPS) * ln_g.astype(jnp.float32) + ln_b.astype(jnp.float32)
    z = jax.nn.silu(zf).astype(a.dtype)
    return z @ w_proj


def s5_branch(u, lam_re, lam_im, log_dt, b_re, b_im, c_re, c_im, d_skip, w_v, w_g):
    bsz, seq_len, _ = u.shape
    uf = u.astype(jnp.float32).reshape(bsz, seq_len, N_SSM_GROUPS, SSM_GROUP)
    lam = lax.complex(lam_re.astype(jnp.float32), lam_im.astype(jnp.float32))
    dt = jnp.exp(log_dt.astype(jnp.float32))[:, None]
    lam_bar = jnp.exp(lam * dt)
    b = lax.complex(b_re.astype(jnp.float32), b_im.astype(jnp.float32))
    b_bar = ((lam_bar - 1.0) / lam)[..., None] * b
    bu = jnp.einsum("blgh,gph->blgp", uf.astype(jnp.complex64), b_bar)
    a = jnp.broadcast_to(lam_bar, bu.shape)

    def combine(e1, e2):
        a1, s1 = e1
        a2, s2 = e2
        return a1 * a2, a2 * s1 + s2

    _, states = lax.associative_scan(combine, (a, bu), axis=1)
    c = lax.complex(c_re.astype(jnp.float32), c_im.astype(jnp.float32))
    y = jnp.einsum("blgp,ghp->blgh", states, c).real \
        + d_skip.astype(jnp.float32).reshape(N_SSM_GROUPS, SSM_GROUP) * uf
    y = jax.nn.gelu(y.reshape(bsz, seq_len, D_SSM)).astype(u.dtype)
    return (y @ w_v) * jax.nn.sigmoid(y @ w_g)


def setup_inputs(seed: int = 0) -> dict:
    key = jax.random.key(seed)
    ks = jax.random.split(key, 32)
    f32 = jnp.float32
    nrm = lambda k, shape, scale: jax.random.normal(k, shape, f32) * scale
    gain = lambda k, shape: 1.0 + 0.02 * jax.random.normal(k, shape, f32)
    G, P, H = N_SSM_GROUPS, SSM_STATE, SSM_GROUP
    n_idx = jnp.arange(P, dtype=f32)
    return {
        "x": nrm(ks[0], (BATCH, SEQ, D_MODEL), 1.0),
        "meta_tokens": nrm(ks[1], (N_META, D_MODEL), 1.0),
        "ffn1_norm": gain(ks[2], (DEPTH, D_MODEL)),
        "ffn1_w1": nrm(ks[3], (DEPTH, D_MODEL, D_FF), D_MODEL ** -0.5),
        "ffn1_w3": nrm(ks[4], (DEPTH, D_MODEL, D_FF), D_MODEL ** -0.5),
        "ffn1_w2": nrm(ks[5], (DEPTH, D_FF, D_MODEL), D_FF ** -0.5),
        "mix_norm": gain(ks[6], (DEPTH, D_MODEL)),
        "w_in": nrm(ks[7], (DEPTH, D_MODEL, D_IN), D_MODEL ** -0.5),
        "b_gate": nrm(ks[8], (DEPTH, N_BRANCHES * D_MODEL), 0.01),
        "conv_dw": nrm(ks[9], (DEPTH, CONV_WIDTH, D_CONV), CONV_WIDTH ** -0.5),
        "conv_dw_b": nrm(ks[10], (DEPTH, D_CONV), 0.01),
        "conv_ln_g": gain(ks[11], (DEPTH, D_CONV)),
        "conv_ln_b": nrm(ks[12], (DEPTH, D_CONV), 0.01),
        "conv_proj": nrm(ks[13], (DEPTH, D_CONV, D_MODEL), D_CONV ** -0.5),
        "ssm_lam_re": -0.5 + 0.01 * jax.random.normal(ks[14], (DEPTH, G, P), f32),
        "ssm_lam_im": math.pi * n_idx + 0.01 * jax.random.normal(ks[15], (DEPTH, G, P), f32),
        "ssm_log_dt": jax.random.uniform(ks[16], (DEPTH, G), f32,
                                         minval=math.log(DT_MIN), maxval=math.log(DT_MAX)),
        "ssm_b_re": nrm(ks[17], (DEPTH, G, P, H), (2.0 * H) ** -0.5),
        "ssm_b_im": nrm(ks[18], (DEPTH, G, P, H), (2.0 * H) ** -0.5),
        "ssm_c_re": nrm(ks[19], (DEPTH, G, H, P), (2.0 * P) ** -0.5),
        "ssm_c_im": nrm(ks[20], (DEPTH, G, H, P), (2.0 * P) ** -0.5),
        "ssm_d": nrm(ks[21], (DEPTH, D_SSM), 1.0),
        "ssm_w_v": nrm(ks[22], (DEPTH, D_SSM, D_MODEL), D_SSM ** -0.5),
        "ssm_w_g": nrm(ks[23], (DEPTH, D_SSM, D_MODEL), D_SSM ** -0.5),
        "w_out": nrm(ks[24], (DEPTH, D_MODEL, D_MODEL), D_MODEL ** -0.5),
        "ffn2_norm": gain(ks[25], (DEPTH, D_MODEL)),
        "ffn2_w1": nrm(ks[26], (DEPTH, D_MODEL, D_FF), D_MODEL ** -0.5),
        "ffn2_w3": nrm(ks[27], (DEPTH, D_MODEL, D_FF), D_MODEL ** -0.5),
        "ffn2_w2": nrm(ks[28], (DEPTH, D_FF, D_MODEL), D_FF ** -0.5),
        "final_norm": gain(ks[29], (D_MODEL,)),
    }


def reference(x, meta_tokens, ffn1_norm, ffn1_w1, ffn1_w3, ffn1_w2, mix_norm, w_in, b_gate,
              conv_dw, conv_dw_b, conv_ln_g, conv_ln_b, conv_proj,
              ssm_lam_re, ssm_lam_im, ssm_log_dt, ssm_b_re, ssm_b_im, ssm_c_re, ssm_c_im,
              ssm_d, ssm_w_v, ssm_w_g, w_out, ffn2_norm, ffn2_w1, ffn2_w3, ffn2_w2, final_norm):
    bsz = x.shape[0]
    meta = jnp.broadcast_to(meta_tokens[None].astype(x.dtype), (bsz, N_META, D_MODEL))
    h = jnp.concatenate([meta, x], axis=1)
    for l in range(DEPTH):
        h = h + 0.5 * swiglu_ffn(rms_norm(h, ffn1_norm[l]), ffn1_w1[l], ffn1_w3[l], ffn1_w2[l])
        u = rms_norm(h, mix_norm[l])
        proj = u @ w_in[l]
        conv_in, ssm_in, gate_in = jnp.split(proj, [2 * D_CONV, 2 * D_CONV + D_SSM], axis=-1)
        y_conv = conformer_conv_branch(conv_in, conv_dw[l], conv_dw_b[l], conv_ln_g[l],
                                       conv_ln_b[l], conv_proj[l])
        y_ssm = s5_branch(ssm_in, ssm_lam_re[l], ssm_lam_im[l], ssm_log_dt[l], ssm_b_re[l],
                          ssm_b_im[l], ssm_c_re[l], ssm_c_im[l], ssm_d[l], ssm_w_v[l], ssm_w_g[l])
        g_conv, g_ssm = jnp.split(jax.nn.sigmoid(gate_in + b_gate[l]), 2, axis=-1)
        h = h + (g_conv * y_conv + g_ssm * y_ssm) @ w_out[l]
        h = h + 0.5 * swiglu_ffn(rms_norm(h, ffn2_norm[l]), ffn2_w1[l], ffn2_w3[l], ffn2_w2[l])
    h = rms_norm(h, final_norm)
    return h[:, N_META:]
```

```python
import math
from contextlib import ExitStack

import numpy as np
import concourse.bass as bass
import concourse.mybir as mybir
from concourse.bass_utils import run_bass_kernel_spmd

F32 = mybir.dt.float32
BF16 = mybir.dt.bfloat16
AF = mybir.ActivationFunctionType
ALU = mybir.AluOpType

NCORES = 8
NT = 2056
TS = 257
NSUB = 8
SPT = 2
NST = NSUB // SPT
D = 1024
DC = 8
DFF = 2816
FC = 22
NCH = 257
G = 32
HALO = 32
EPS = 1e-6
ARENA = 53200


class Tok:
    __slots__ = ("sem", "val", "eng")

    def __init__(self, sem, val, eng):
        self.sem, self.val, self.eng = sem, val, eng


class Prog:
    def __init__(self, nc, es):
        self.nc, self.es = nc, es
        self.engs = {"pe": nc.tensor, "act": nc.scalar, "dve": nc.vector, "pool": nc.gpsimd, "sp": nc.sync}
        self.ops = {e: [] for e in self.engs}
        self.cur = {}
        self.waited = {e: {} for e in self.engs}
        self.lastw = {}
        self.readers = {}
        self.pend_r, self.pend_w = [], []
        self.nsem = 0
        self.dsems = []

    def newsem(self, name):
        self.nsem += 1
        return self.es.enter_context(self.nc.semaphore(f"{name}{self.nsem}"))

    def dsem(self, name="d"):
        d = [self.newsem(name), 0]
        self.dsems.append(d)
        return d

    def engsem(self, e):
        c = self.cur.get(e)
        if c is None or c[1] >= 30000:
            c = [self.newsem("s" + e), 0]
            self.cur[e] = c
        return c

    def _need(self, e, toks):
        need = {}
        for t in toks:
            if e == "pe" and t.eng == "pe":
                continue
            sid = id(t.sem)
            if t.val <= self.waited[e].get(sid, 0):
                continue
            if sid not in need or need[sid].val < t.val:
                need[sid] = t
        for sid, t in need.items():
            self.waited[e][sid] = t.val
        return list(need.values())

    def op(self, e, fn, reads=(), writes=(), signal=True, dsem=None, inc=1):
        toks = []
        for k in reads:
            t = self.lastw.get(k)
            if t is not None:
                toks.append(t)
        for k in writes:
            t = self.lastw.get(k)
            if t is not None:
                toks.append(t)
            toks.extend(self.readers.get(k, {}).values())
        waits = self._need(e, toks)
        tok, sig = None, None
        if dsem is not None:
            dsem[1] += 16
            tok = Tok(dsem[0], dsem[1], "dma")
            sig = (dsem[0], 16)
        elif signal:
            c = self.engsem(e)
            c[1] += inc
            tok = Tok(c[0], c[1], e)
            sig = (c[0], inc)
        self.ops[e].append((waits, fn, sig))
        if e == "pe" and tok is None:
            self.pend_r.extend(reads)
            self.pend_w.extend(writes)
            return None
        if e == "pe":
            reads = list(reads) + self.pend_r
            writes = list(writes) + self.pend_w
            self.pend_r, self.pend_w = [], []
        for k in writes:
            self.lastw[k] = tok
            self.readers[k] = {}
        for k in reads:
            r = self.readers.setdefault(k, {})
            sid = id(tok.sem)
            if sid not in r or r[sid].val < tok.val:
                r[sid] = tok
        return tok

    def barrier(self):
        toks = [Tok(c[0], c[1], e) for e, c in self.cur.items() if c[1] > 0]
        toks += [Tok(d[0], d[1], "dma") for d in self.dsems if d[1] > 0]
        for e in self.engs:
            w = []
            for t in toks:
                sid = id(t.sem)
                if t.val > self.waited[e].get(sid, 0):
                    self.waited[e][sid] = t.val
                    w.append(t)
            if w:
                self.ops[e].append((w, None, None))

    def emit(self):
        with self.nc.Block() as block:
            def mk(e):
                def f(eng):
                    for waits, fn, sig in self.ops[e]:
                        for t in waits:
                            eng.wait_ge(t.sem, t.val)
                        if fn is None:
                            continue
                        ins = fn(eng)
                        if sig is not None:
                            ins.then_inc(sig[0], sig[1])
                return f
            block.tensor(mk("pe"))
            block.scalar(mk("act"))
            block.vector(mk("dve"))
            block.gpsimd(mk("pool"))
            block.sync(mk("sp"))


class Arena:
    def __init__(self, t, size, base=0):
        self.t, self.size, self.off = t, size, base

    def mark(self):
        return self.off

    def release(self, m):
        self.off = m

    def f32(self, *shape):
        n = int(np.prod(shape))
        assert self.off + n <= self.size, f"arena overflow {self.off}+{n}"
        ap = self.t[:, self.off:self.off + n]
        self.off += n
        return _shape(ap, shape)

    def bf16(self, *shape):
        n = int(np.prod(shape))
        nf = (n + 1) // 2
        assert self.off + nf <= self.size, f"arena overflow {self.off}+{nf}"
        ap = self.t[:, self.off:self.off + nf].bitcast(BF16)[:, 0:n]
        self.off += nf
        return _shape(ap, shape)


def _shape(ap, shape):
    if len(shape) == 1:
        return ap
    if len(shape) == 2:
        return ap.rearrange("p (a b) -> p a b", a=shape[0])
    if len(shape) == 3:
        return ap.rearrange("p (a b c) -> p a b c", a=shape[0], b=shape[1])
    raise ValueError(shape)


class WStream:
    def __init__(self, P, ar, name, nslots, kc, ncol, group=1):
        self.P, self.name, self.n, self.group = P, name, nslots, group
        self.slots = [ar.bf16(kc, ncol) for _ in range(nslots)]
        self.sems = [P.dsem("w" + name) for _ in range(nslots)]
        self.units, self.issued, self.consumed = [], 0, 0

    def plan(self, units):
        self.units.extend(units)

    def _issue(self):
        i = self.issued
        s = i % self.n
        src, kc, ncol = self.units[i]
        dst = self.slots[s][:, 0:kc, 0:ncol].rearrange("p a b -> p (a b)")
        if kc * ncol > 2048:
            dst = dst.rearrange("p (a b) -> p a b", a=2)
            src = src.rearrange("p (a b) -> p a b", a=2)
        self.P.op("pool", lambda e, d=dst, r=src: e.dma_start(out=d, in_=r),
                  writes=((self.name, s),), dsem=self.sems[s])
        self.issued += 1

    def prefetch(self):
        while self.issued < len(self.units) and self.issued < self.consumed + self.n - (self.group - 1):
            self._issue()

    def get(self):
        self.prefetch()
        i = self.consumed
        s = i % self.n
        _, kc, ncol = self.units[i]
        self.consumed += 1
        return self.slots[s][:, 0:kc, 0:ncol], (self.name, s)


def build(nc):
    dram = {}

    def din(name, shape):
        dram[name] = nc.dram_tensor(name, list(shape), F32, kind="ExternalInput").ap()
        return dram[name]

    xin = din("xin", (NT, D))
    w1a = [din(f"ffn{i}_w1", (FC, 128, DC * 128)) for i in (1, 2)]
    w3a = [din(f"ffn{i}_w3", (FC, 128, DC * 128)) for i in (1, 2)]
    w2a = [din(f"ffn{i}_w2", (DC, 128, FC * 128)) for i in (1, 2)]
    w_in = din("w_in", (28, 128, DC * 128))
    conv_proj = din("conv_proj", (DC, 128, 4 * 128))
    w_v = din("ssm_w_v", (DC, 128, 4 * 128))
    w_g = din("ssm_w_g", (DC, 128, 4 * 128))
    w_out = din("w_out", (DC, 128, DC * 128))
    vecs_d = din("vecs", (128, NVEC))
    sp_d = din("sp", (128, NSP))
    cst_d = din("cst", (128, 4 * 128))
    esel_d = din("esel", (128, 64 * 128))
    out_d = nc.dram_tensor("out", [NT, D], F32, kind="ExternalOutput").ap()
    cc_in = nc.dram_tensor("cc_in", [128, 192], F32)
    cc_out = nc.dram_tensor("cc_out", [2 * 128, 192], F32)

    es = ExitStack()
    with es:
        es.enter_context(nc.allow_low_precision("bf16 matmul operands, fp32 accumulation"))
        es.enter_context(nc.allow_non_contiguous_dma("layout"))
        arena_t = es.enter_context(nc.sbuf_tensor("arena", [128, ARENA], F32))
        banks = [es.enter_context(nc.psum_tensor(f"ps{i}", [128, 512], F32)) for i in range(8)]
        P = Prog(nc, es)
        ar = Arena(arena_t, ARENA)

        def wr(x):
            return dram[x] if isinstance(x, str) else x

        hres = ar.f32(DC, NT)
        ufm = ar.bf16(4, NT)
        zall = ar.bf16(4, HALO + NT)
        vecs = ar.f32(NVEC)
        ident = ar.f32(128)
        onesb = ar.bf16(128)
        rstd = ar.f32(TS)

        csem = P.dsem("c")
        DBG = _DBG
        dbg = {}
        dbgsem = P.dsem("g")

        def dump(name, ap3):
            if not DBG:
                return
            shp = list(ap3.shape)
            dbg[name] = nc.dram_tensor("dbg_" + name, shp, F32, kind="ExternalOutput").ap()
            P.barrier()
            P.op("pool", lambda e: e.dma_start(out=dbg[name], in_=ap3), dsem=dbgsem)
            P.barrier()
        P.op("sp", lambda e: e.dma_start(out=vecs, in_=vecs_d), writes=("vecs",), dsem=csem)
        P.op("sp", lambda e: e.dma_start(out=ident, in_=cst_d[:, 0:128]), writes=("ident",), dsem=csem)
        P.op("dve", lambda e: e.memset(onesb, 1.0), writes=("ones",))
        P.op("dve", lambda e: e.memset(zall[:, :, 0:HALO], 0.0), writes=("zhalo",))
        P.barrier()

        V_F1N, V_MIXN, V_F2N, V_FIN, V_BG, V_DWB, V_LNG, V_LNB, V_DW = 0, 8, 16, 24, 32, 48, 52, 56, 60

        psrr = [0]

        def bank(pool):
            b = pool[psrr[0] % len(pool)]
            psrr[0] += 1
            return b

        def mm_group(bk, out_ap, terms, extra_w=()):
            n = len(terms)
            for i, (lhsT, rhs, rk) in enumerate(terms):
                last = i == n - 1
                P.op("pe", lambda e, o=out_ap, l=lhsT, r=rhs, st=(i == 0), sp_=last:
                     e.matmul(o, l, r, start=st, stop=sp_),
                     reads=rk, writes=(("ps", bk),) if (i == 0 or last) else (), signal=last)

        TOPN = 7500
        top = Arena(arena_t, ARENA, base=ARENA - TOPN)
        ar.size = ARENA - TOPN
        wst3 = top.bf16(G, 192)
        mbf = top.bf16(G, 128)
        woutb = top.bf16(G, 128)
        CA16 = top.f32(64)
        CB16 = top.f32(64)
        CA = top.f32(64)
        CB = top.f32(64)
        onehot = top.f32(8)
        dcol = top.f32(G)
        m0 = ar.mark()
        xblk = [ar.f32(D) for _ in range(2)]
        xsem = [P.dsem("x") for _ in range(2)]
        nblk = (NT + 127) // 128
        def in_block(bi):
            r0 = bi * 128
            nr = min(128, NT - r0)
            s = bi % 2
            P.op("sp", lambda e, s=s, r0=r0, nr=nr: e.dma_start(out=xblk[s][0:nr, :], in_=xin[r0:r0 + nr, :]),
                 writes=(("xblk", s),), dsem=xsem[s])
            for hb in range(2):
                bk = 6 + hb
                for kk in range(4):
                    k = hb * 4 + kk
                    P.op("pe", lambda e, bk=bk, kk=kk, k=k, s=s, nr=nr:
                         e.transpose(banks[bk][:, kk * 128:kk * 128 + nr], xblk[s][0:nr, k * 128:(k + 1) * 128],
                                     ident[0:nr, 0:nr]),
                         reads=(("xblk", s),), writes=(("ps", bk),) if kk in (0, 3) else (), signal=(kk == 3))
                eng = "act" if hb == 0 else "dve"
                src = banks[bk][:, :].rearrange("p (a b) -> p a b", a=4)[:, :, 0:nr]
                dst = hres[:, hb * 4:hb * 4 + 4, r0:r0 + nr]
                if eng == "act":
                    P.op("act", lambda e, d=dst, s_=src: e.activation(d, s_, AF.Copy),
                         reads=(("ps", bk),), writes=(("hres", bi),))
                else:
                    P.op("dve", lambda e, d=dst, s_=src: e.tensor_copy(d, s_),
                         reads=(("ps", bk),), writes=(("hres", bi),))
        inq = list(range(nblk))

        def in_fill():
            if inq:
                in_block(inq.pop(0))

        in_fill()
        in_fill()
        mask = ar.f32(128)
        adi = ar.f32(128)
        spt = ar.f32(NSP)
        ssem = P.dsem("s")
        P.op("sp", lambda e: e.dma_start(out=mask, in_=cst_d[:, 128:256]), writes=("prep",), dsem=ssem)
        P.op("sp", lambda e: e.dma_start(out=adi, in_=cst_d[:, 256:384]), writes=("prep",), dsem=ssem)
        P.op("sp", lambda e: e.dma_start(out=spt, in_=sp_d), writes=("prep",), dsem=ssem)
        P.op("dve", lambda e: e.tensor_copy(dcol, spt[:, 96:128]), reads=("prep",), writes=("prep",))
        P.op("dve", lambda e: e.tensor_copy(onehot, spt[:, 128:136]), reads=("prep",), writes=("prep",))
        lre, lim, ldt = spt[:, 0:32], spt[:, 32:64], spt[:, 64:96]
        o0 = 136
        Bre = spt[:, o0:o0 + 512].rearrange("p (g h) -> p g h", g=G)
        Bim = spt[:, o0 + 512:o0 + 1024].rearrange("p (g h) -> p g h", g=G)
        Cre = spt[:, o0 + 1024:o0 + 1536].rearrange("p (g h) -> p g h", g=G)
        Cim = spt[:, o0 + 1536:o0 + 2048].rearrange("p (g h) -> p g h", g=G)

        pcnt = [0]

        def tick():
            pcnt[0] += 1
            if pcnt[0] % 14 == 0:
                in_fill()

        def V(fn):
            P.op("dve", fn, reads=("prep",), writes=("prep",))
            tick()

        def A(fn):
            P.op("act", fn, reads=("prep",), writes=("prep",))
            tick()

        def tt(o, a, b, op):
            V(lambda e: e.tensor_tensor(o, a, b, op))

        def ts(o, a, s1, s2, op0, op1=None):
            if op1 is None:
                V(lambda e: e.tensor_scalar(o, a, s1, None, op0))
            else:
                V(lambda e: e.tensor_scalar(o, a, s1, s2, op0, op1))

        def tmpg(n=1):
            return [ar.f32(G) for _ in range(n)]

        dt_, ar_, ai_, mag, imag, s1, c1, wtmp = tmpg(8)
        A(lambda e: e.activation(dt_, ldt, AF.Exp))
        tt(ar_, lre, dt_, ALU.mult)
        tt(ai_, lim, dt_, ALU.mult)
        A(lambda e: e.activation(mag, ar_, AF.Exp))
        A(lambda e: e.activation(imag, ar_, AF.Exp, scale=-1.0))
        TWO_PI = 2.0 * math.pi
        MAGIC = 12582912.0
        wn, = tmpg(1)

        def sin_of(o, x, shift):
            ts(wtmp, x, shift, None, ALU.add)
            ts(wn, wtmp, 1.0 / TWO_PI, MAGIC, ALU.mult, ALU.add)
            ts(wn, wn, -MAGIC, None, ALU.add)
            V(lambda e: e.scalar_tensor_tensor(wtmp, wn, -TWO_PI, wtmp, ALU.mult, ALU.add))
            ts(wn, wtmp, math.pi, TWO_PI, ALU.is_gt, ALU.mult)
            tt(wtmp, wtmp, wn, ALU.subtract)
            ts(wn, wtmp, -math.pi, TWO_PI, ALU.is_lt, ALU.mult)
            tt(wtmp, wtmp, wn, ALU.add)
            A(lambda e: e.activation(o, wtmp, AF.Sin))

        sin_of(s1, ai_, 0.0)
        sin_of(c1, ai_, 0.5 * math.pi)
        L1r, L1i, N1r, N1i = tmpg(4)
        tt(L1r, mag, c1, ALU.mult)
        tt(L1i, mag, s1, ALU.mult)
        tt(N1r, imag, c1, ALU.mult)
        V(lambda e: e.scalar_tensor_tensor(N1i, imag, -1.0, s1, ALU.mult, ALU.mult))
        PWr = ar.f32(16, G)
        PWi = ar.f32(16, G)
        V(lambda e: e.memset(PWr[:, 7, :], 1.0))
        V(lambda e: e.memset(PWi[:, 7, :], 0.0))
        t1, t2 = tmpg(2)

        def cmul(orr, oi, ar0, ai0, br0, bi0):
            tt(t1, ar0, br0, ALU.mult)
            tt(t2, ai0, bi0, ALU.mult)
            tt(orr, t1, t2, ALU.subtract)
            tt(t1, ar0, bi0, ALU.mult)
            tt(t2, ai0, br0, ALU.mult)
            tt(oi, t1, t2, ALU.add)

        for k in range(1, 9):
            cmul(PWr[:, 7 + k, :], PWi[:, 7 + k, :], PWr[:, 6 + k, :], PWi[:, 6 + k, :], L1r, L1i)
        for k in range(1, 8):
            cmul(PWr[:, 7 - k, :], PWi[:, 7 - k, :], PWr[:, 8 - k, :], PWi[:, 8 - k, :], N1r, N1i)
        nr_, den, cr, ci = tmpg(4)
        ts(nr_, L1r, -1.0, None, ALU.add)
        tt(t1, lre, lre, ALU.mult)
        tt(t2, lim, lim, ALU.mult)
        tt(den, t1, t2, ALU.add)
        V(lambda e: e.reciprocal(den, den))
        tt(t1, nr_, lre, ALU.mult)
        tt(t2, L1i, lim, ALU.mult)
        tt(cr, t1, t2, ALU.add)
        tt(cr, cr, den, ALU.mult)
        tt(t1, L1i, lre, ALU.mult)
        tt(t2, nr_, lim, ALU.mult)
        tt(ci, t1, t2, ALU.subtract)
        tt(ci, ci, den, ALU.mult)
        Bbr = ar.f32(G, 16)
        Bbi = ar.f32(G, 16)
        T1 = ar.f32(G, 16)
        T2 = ar.f32(G, 16)

        def bc(x):
            return x.unsqueeze(2).broadcast_to([128, G, 16])

        tt(T1, Bre, bc(cr), ALU.mult)
        tt(T2, Bim, bc(ci), ALU.mult)
        tt(Bbr, T1, T2, ALU.subtract)
        tt(T1, Bim, bc(cr), ALU.mult)
        tt(T2, Bre, bc(ci), ALU.mult)
        tt(Bbi, T1, T2, ALU.add)
        UA = ar.f32(8, G)
        UB = ar.f32(8, G)
        lo, hi = slice(0, 64), slice(64, 128)

        def neg(o, a):
            ts(o, a, -1.0, None, ALU.mult)

        def cp(o, a):
            V(lambda e: e.tensor_copy(o, a))

        cp(UA[lo], PWr[lo, 7:15, :])
        cp(UA[hi], PWi[hi, 7:15, :])
        neg(UB[lo], PWi[lo, 7:15, :])
        cp(UB[hi], PWr[hi, 7:15, :])
        BL = ar.f32(G, 8, 16)
        for kb in range(8):
            tt(T1, Bbr, bc(UA[:, kb, :]), ALU.mult)
            tt(T2, Bbi, bc(UB[:, kb, :]), ALU.mult)
            tt(BL[:, :, kb, :], T1, T2, ALU.add)
        WA = ar.f32(8, G)
        WB = ar.f32(8, G)
        CL = ar.f32(G, 8, 16)

        def build_cl(i0):
            cp(WA[lo], PWr[lo, i0:i0 + 8, :])
            neg(WA[hi], PWi[hi, i0:i0 + 8, :])
            neg(WB[lo], PWi[lo, i0:i0 + 8, :])
            neg(WB[hi], PWr[hi, i0:i0 + 8, :])
            for t in range(8):
                tt(T1, Cre, bc(WA[:, t, :]), ALU.mult)
                tt(T2, Cim, bc(WB[:, t, :]), ALU.mult)
                tt(CL[:, :, t, :], T1, T2, ALU.add)

        build_cl(8)
        cp(woutb, CL.rearrange("p g t h -> p g (t h)"))
        build_cl(0)
        a8, b8 = PWr[:, 15, :], PWi[:, 15, :]
        cp(CA[:, 0:32], a8)
        cp(CA[:, 32:64], a8)
        neg(CB[lo, 0:32], b8[lo])
        cp(CB[hi, 0:32], b8[hi])
        cp(CB[lo, 32:64], b8[lo])
        neg(CB[hi, 32:64], b8[hi])
        q_r, q_i, q_r2, q_i2 = tmpg(4)
        cp(q_r, a8)
        cp(q_i, b8)
        for _ in range(4):
            cmul(q_r2, q_i2, q_r, q_i, q_r, q_i)
            cp(q_r, q_r2)
            cp(q_i, q_i2)
        cp(CA16[:, 0:32], q_r)
        cp(CA16[:, 32:64], q_r)
        neg(CB16[lo, 0:32], q_i[lo])
        cp(CB16[hi, 0:32], q_i[hi])
        cp(CB16[lo, 32:64], q_i[lo])
        neg(CB16[hi, 32:64], q_i[hi])
        BLf = BL.rearrange("p g k h -> p g (k h)")
        CLf = CL.rearrange("p g t h -> p g (t h)")
        mtmp = ar.f32(128)
        for g in range(G):
            bk = 4 + g % 2
            P.op("pe", lambda e, g=g, bk=bk: e.matmul(banks[bk][:, 0:128], BLf[:, g, :], CLf[:, g, :], start=True, stop=True),
                 reads=("prep",), writes=(("ps", bk),))
            P.op("dve", lambda e, bk=bk: e.tensor_tensor(mtmp, banks[bk][:, 0:128], mask, ALU.mult),
                 reads=(("ps", bk),), writes=("mtmp",))
            P.op("dve", lambda e, g=g: e.scalar_tensor_tensor(mbf[:, g, :], adi, dcol[:, g:g + 1], mtmp, ALU.mult, ALU.add),
                 reads=("mtmp",), writes=("mbf",))
            bt = 6 + g % 2
            P.op("pe", lambda e, g=g, bt=bt: e.transpose(banks[bt][:, 0:128], BLf[:, g, :], ident),
                 reads=("prep",), writes=(("ps", bt),))
            P.op("act", lambda e, g=g, bt=bt: e.activation(wst3[:, g, 0:128], banks[bt][:, 0:128], AF.Copy),
                 reads=(("ps", bt),), writes=("wst3",))
            P.op("act", lambda e, g=g, bt=bt: e.activation(wst3[:, g, 128:192], banks[bt][:, 0:64], AF.Copy),
                 reads=(("ps", bt),), writes=("wst3",))
        while inq:
            in_fill()
        P.barrier()
        ar.release(m0)
        dump("h0", hres)

        def hkey(sub):
            return ("h", sub)

        def cols(sub):
            return slice(sub * TS, (sub + 1) * TS)

        def rmsnorm_to(sub, gcol, xn, xnkey, sq):
            c = cols(sub)
            P.op("act", lambda e: e.activation(sq, hres[:, :, c], AF.Square), reads=(hkey(sub),), writes=("sq",))
            bk = 7
            mm_group(bk, banks[bk][:, 0:TS], [(onesb, sq[:, k, :], ("sq",)) for k in range(DC)])
            P.op("dve", lambda e: e.tensor_scalar(rstd, banks[bk][:, 0:TS], 1.0 / D, EPS, ALU.mult, ALU.add),
                 reads=(("ps", bk),), writes=("rstd",))
            P.op("act", lambda e: e.activation(rstd, rstd, AF.Sqrt), reads=("rstd",), writes=("rstd",))
            P.op("dve", lambda e: e.reciprocal(rstd, rstd), reads=("rstd",), writes=("rstd",))
            for k in range(DC):
                P.op("dve", lambda e, k=k: e.scalar_tensor_tensor(xn[:, k, :], hres[:, k, c], vecs[:, gcol + k:gcol + k + 1],
                                                                  rstd, ALU.mult, ALU.mult),
                     reads=(hkey(sub), "rstd"), writes=(xnkey,))

        def ffn(st, which, xn, hid, sq, sil, w13, w2s, fill=None):
            gcol = V_F1N if which == 0 else V_F2N
            subs = [st * SPT + i for i in range(SPT)]
            for i, sub in enumerate(subs):
                rmsnorm_to(sub, gcol, xn[i], ("xn", i), sq)
            w2s.prefetch()
            for f in range(FC):
                wa, ka = w13.get()
                wb, kb = w13.get()
                for i, sub in enumerate(subs):
                    b1 = bank([0, 1, 2, 3, 4, 5])
                    b3 = bank([0, 1, 2, 3, 4, 5])
                    mm_group(b1, banks[b1][:, 0:TS], [(wa[:, k, :], xn[i][:, k, :], (ka, ("xn", i))) for k in range(DC)])
                    mm_group(b3, banks[b3][:, 0:TS], [(wb[:, k, :], xn[i][:, k, :], (kb, ("xn", i))) for k in range(DC)])
                    sl = sil[(f * SPT + i) % 2]
                    skey = ("sil", (f * SPT + i) % 2)
                    P.op("act", lambda e, sl=sl, b1=b1: e.activation(sl, banks[b1][:, 0:TS], AF.Silu),
                         reads=(("ps", b1),), writes=(skey,))
                    P.op("dve", lambda e, sl=sl, b3=b3, i=i, f=f: e.tensor_tensor(hid[i][:, f, :], sl, banks[b3][:, 0:TS], ALU.mult),
                         reads=(skey, ("ps", b3)), writes=(("hid", i), "gh") if f == 0 else (("hid", i),))
                    if fill is not None:
                        fill(3)
            for d in range(DC):
                if fill is not None:
                    fill(8)
                wc, kc_ = w2s.get()
                for i, sub in enumerate(subs):
                    bo = bank([0, 1, 2, 3])
                    mm_group(bo, banks[bo][:, 0:TS], [(wc[:, f, :], hid[i][:, f, :], (kc_, ("hid", i), "gh")) for f in range(FC)])
                    c = cols(sub)
                    P.op("dve", lambda e, bo=bo, d=d, c=c: e.scalar_tensor_tensor(hres[:, d, c], banks[bo][:, 0:TS], 0.5,
                                                                                  hres[:, d, c], ALU.mult, ALU.add),
                         reads=(("ps", bo), hkey(sub)), writes=(hkey(sub),))

        def w13_units(which):
            u = []
            for f in range(FC):
                u.append((w1a[which][f], DC, 128))
                u.append((w3a[which][f], DC, 128))
            return u

        def w2_units(which):
            return [(w2a[which][d], FC, 128) for d in range(DC)]

        def wcols_units(w, kc, colchunks):
            return [(w[o], kc, 128) for o in colchunks]

        m1 = ar.mark()
        xn = [ar.bf16(DC, TS) for _ in range(SPT)]
        hid = [ar.bf16(FC, TS) for _ in range(SPT)]
        sq = ar.bf16(DC, TS)
        sil = [ar.f32(TS) for _ in range(2)]
        w13 = WStream(P, ar, "w13", 4, DC, 128, group=2)
        w2s = WStream(P, ar, "w2", 3, FC, 128)
        wk8 = WStream(P, ar, "wk8", 4, DC, 128, group=2)
        for st in range(NST):
            w13.plan(w13_units(0))
            w2s.plan(w2_units(0))
            wk8.plan(wcols_units(w_in, DC, [8, 9, 10, 11]))
            cu = []
            for q in range(4):
                cu += wcols_units(w_in, DC, [q, 4 + q])
            wk8.plan(cu)
        for st in range(NST):
            subs = [st * SPT + i for i in range(SPT)]
            ffn(st, 0, xn, hid, sq, sil, w13, w2s)
            wk8.prefetch()
            for i, sub in enumerate(subs):
                rmsnorm_to(sub, V_MIXN, xn[i], ("xn", i), sq)
            for q in range(4):
                wu, ku = wk8.get()
                for i, sub in enumerate(subs):
                    b = bank([0, 1, 2, 3, 4, 5])
                    mm_group(b, banks[b][:, 0:TS], [(wu[:, k, :], xn[i][:, k, :], (ku, ("xn", i))) for k in range(DC)])
                    P.op("act", lambda e, b=b, q=q, sub=sub: e.activation(ufm[:, q, cols(sub)], banks[b][:, 0:TS], AF.Copy),
                         reads=(("ps", b),), writes=(("ufm", sub),))
            for q in range(4):
                wv_, kv = wk8.get()
                wg_, kg = wk8.get()
                for i, sub in enumerate(subs):
                    bv = bank([0, 1, 2, 3, 4, 5])
                    bg = bank([0, 1, 2, 3, 4, 5])
                    mm_group(bv, banks[bv][:, 0:TS], [(wv_[:, k, :], xn[i][:, k, :], (kv, ("xn", i))) for k in range(DC)])
                    mm_group(bg, banks[bg][:, 0:TS], [(wg_[:, k, :], xn[i][:, k, :], (kg, ("xn", i))) for k in range(DC)])
                    sl = sil[(q * SPT + i) % 2]
                    skey = ("sil", (q * SPT + i) % 2)
                    P.op("act", lambda e, sl=sl, bg=bg: e.activation(sl, banks[bg][:, 0:TS], AF.Sigmoid),
                         reads=(("ps", bg),), writes=(skey,))
                    zc = slice(HALO + sub * TS, HALO + (sub + 1) * TS)
                    P.op("dve", lambda e, sl=sl, bv=bv, q=q, zc=zc: e.tensor_tensor(zall[:, q, zc], sl, banks[bv][:, 0:TS], ALU.mult),
                         reads=(skey, ("ps", bv)), writes=(("z", sub),))
        P.barrier()
        dump("h1", hres)
        dump("u", ufm)
        dump("z", zall)
        ar.release(m1)

        m2 = ar.mark()
        eraw = ar.f32(4096)
        esel = eraw.bitcast(BF16).rearrange("p (a b) -> p a b", a=64)
        Pst = ar.f32(64)
        W2t = ar.f32(64)
        esem = P.dsem("e")
        P.op("pool", lambda e: e.dma_start(out=esel, in_=esel_d.rearrange("p (a b) -> p a b", a=64)),
             writes=("esel",), dsem=esem)
        P.barrier()

        ug = ar.bf16(G, NCH)
        vxs = ar.bf16(G, NCH + 1)
        vy = ar.bf16(G, NCH)
        gz = ar.f32(6 * TS)
        gath = gz[:, 0:2 * 192].rearrange("p (a b) -> p a b", a=2)
        gtmp = [[gz[:, (3 * a + b) * TS:(3 * a + b + 1) * TS] for b in range(3)] for a in range(2)]
        pay = ar.f32(192)
        acc = ar.f32(192)
        NB, KB = 16, 16
        Rall = ar.f32(2, G, NB + 1)
        ucont = vy.rearrange("p g c -> p (g c)").rearrange("p (q j c) -> p q j c", q=4, j=8)
        for q in range(4):
            for j in range(8):
                if (q * 8 + j) % 2 == 0:
                    P.op("act", lambda e, q=q, j=j: e.activation(ucont[:, q, j, :], ufm[:, q, j::8], AF.Copy), writes=("vy",))
                else:
                    P.op("dve", lambda e, q=q, j=j: e.tensor_copy(ucont[:, q, j, :], ufm[:, q, j::8]), writes=("vy",))
        for g in range(G):
            q, gl = g // 8, g % 8
            bk = bank([0, 1, 2, 3, 4, 5])
            terms = []
            for kb in range(8):
                j = 7 - kb
                terms.append((esel[:, gl * 8 + kb, :], ucont[:, q, j, :], ("vy",)))
            mm_group(bk, banks[bk][:, 0:NCH], terms)
            eng = "act" if g % 2 == 0 else "dve"
            if eng == "act":
                P.op("act", lambda e, g=g, bk=bk: e.activation(ug[:, g, :], banks[bk][:, 0:NCH], AF.Copy),
                     reads=(("ps", bk),), writes=(("ug", g),))
            else:
                P.op("dve", lambda e, g=g, bk=bk: e.tensor_copy(ug[:, g, :], banks[bk][:, 0:NCH]),
                     reads=(("ps", bk),), writes=(("ug", g),))
        for g in range(G):
            bx = bank([4, 5, 6, 7])
            by = bank([4, 5, 6, 7])
            mm_group(bx, banks[bx][:, 0:NCH], [(wst3[:, g, 0:128], ug[:, g, :], (("ug", g),))])
            mm_group(by, banks[by][:, 0:NCH], [(wst3[:, g, 64:192], ug[:, g, :], (("ug", g),))])
            P.op("act", lambda e, g=g, bx=bx: e.activation(vxs[:, g, 1:NCH + 1], banks[bx][:, 0:NCH], AF.Copy),
                 reads=(("ps", bx),), writes=("vxs",))
            P.op("dve", lambda e, g=g, by=by: e.tensor_copy(vy[:, g, :], banks[by][:, 0:NCH]),
                 reads=(("ps", by),), writes=("vy",))
        P.barrier()
        Lst = eraw[:, 0:1024]
        W2b = eraw[:, 1024:2048]
        CAr = eraw[:, 2048:3072]
        CBr = eraw[:, 3072:4096]
        pstr = list(Pst.ap[0])

        def v4(a):
            return a.rearrange("p (a g b) -> p a g b", a=2, g=G)

        def v3(a):
            return a.rearrange("p (a n) -> p a n", a=2)

        def swp(a):
            return bass.AP(a.tensor, a.offset + 512, [list(a.ap[0]), [-512, 2], [1, 512]])

        S = lambda fn, r=("scan",), w=("scan",), eng="dve": P.op(eng, fn, reads=r, writes=w)
        S(lambda e: e.tensor_copy(CAr.rearrange("p (n b) -> p n b", b=NB), CA.unsqueeze(2).broadcast_to([128, 64, NB])))
        S(lambda e: e.tensor_copy(CBr.rearrange("p (n b) -> p n b", b=NB), CB.unsqueeze(2).broadcast_to([128, 64, NB])))
        S(lambda e: e.memset(Lst, 0.0))
        for j in range(KB):
            S(lambda e: e.tensor_tensor(v3(W2b), v3(CBr), swp(Lst), ALU.mult))
            S(lambda e: e.tensor_tensor(Lst, Lst, CAr, ALU.mult), r=("scan", "lwb"))
            S(lambda e: e.tensor_tensor(Lst, Lst, W2b, ALU.add))
            S(lambda e, j=j: e.tensor_tensor(v4(Lst)[:, 0], v4(Lst)[:, 0], vxs[:, :, 1 + j:1 + j + (NB - 1) * KB + 1:KB], ALU.add),
              r=("scan", "vxs", "vy"))
            S(lambda e, j=j: e.tensor_tensor(v4(Lst)[:, 1], v4(Lst)[:, 1], vy[:, :, j:j + (NB - 1) * KB + 1:KB], ALU.add))
            P.op("act", lambda e, j=j: e.activation(vxs[:, :, 1 + j:1 + j + (NB - 1) * KB + 1:KB], v4(Lst)[:, 0], AF.Copy),
                 reads=("scan",), writes=("lwb", "vxs"))
        Rv = Rall
        Pv = Pst.rearrange("p (a b) -> p a b", a=2)
        W2v = W2t.rearrange("p (a b) -> p a b", a=2)
        CBv = CB.rearrange("p (a b) -> p a b", a=2)
        CAv = CA.rearrange("p (a b) -> p a b", a=2)
        CB16v = CB16.rearrange("p (a b) -> p a b", a=2)
        CA16v = CA16.rearrange("p (a b) -> p a b", a=2)
        RSTR = G * (NB + 1)

        def rsw(b):
            r0 = Rall[:, 0, :, b]
            return bass.AP(r0.tensor, r0.offset + RSTR, [list(r0.ap[0]), [-RSTR, 2], [NB + 1, G]])

        def block_scan(init):
            if init is None:
                S(lambda e: e.memset(Rall[:, :, :, 0], 0.0))
            else:
                S(lambda e: e.tensor_copy(Rall[:, :, :, 0], init.rearrange("p (a b) -> p a b", a=2)), r=("scan", "acc"))
            for b in range(NB):
                S(lambda e, b=b: e.tensor_tensor(W2v, CB16v, rsw(b), ALU.mult))
                S(lambda e, b=b: e.tensor_tensor(Pv, CA16v, Rall[:, :, :, b], ALU.mult))
                S(lambda e: e.tensor_tensor(Pst, Pst, W2t, ALU.add))
                S(lambda e, b=b: e.tensor_tensor(Rall[:, :, :, b + 1], Pv, v4(Lst)[:, :, :, b], ALU.add))

        block_scan(None)
        S(lambda e: e.tensor_tensor(W2v, CBv, rsw(NB), ALU.mult))
        S(lambda e: e.tensor_tensor(Pv, CAv, Rall[:, :, :, NB], ALU.mult))
        S(lambda e: e.tensor_tensor(Pst, Pst, W2t, ALU.add))
        S(lambda e: e.tensor_tensor(Pst[:, 0:32], Pst[:, 0:32], vxs[:, :, NCH], ALU.add), r=("scan", "vxs"))
        S(lambda e: e.tensor_tensor(Pst[:, 32:64], Pst[:, 32:64], vy[:, :, NCH - 1], ALU.add), r=("scan", "vy"))
        P.op("dve", lambda e: e.tensor_copy(pay[:, 0:64], Pst), reads=("scan",), writes=("pay",))
        P.op("dve", lambda e: e.tensor_copy(pay[:, 64:192].rearrange("p (a b) -> p a b", a=4), zall[:, :, NT:NT + HALO]),
             reads=(("z", NSUB - 1),), writes=("pay",))
        xs = P.dsem("xc")
        P.op("pool", lambda e: e.dma_start(out=cc_in[:, :], in_=pay), reads=("pay",), writes=("ccin",), dsem=xs)
        P.op("pool", lambda e: e.collective_compute("AllGather", ALU.bypass, replica_groups=[[2 * b, 2 * b + 1] for b in range(NCORES // 2)],
                                                    ins=[cc_in.ap().opt()], outs=[cc_out.ap().opt()]),
             reads=("ccin",), writes=("ccout",))
        P.op("pool", lambda e: e.dma_start(out=gath, in_=cc_out.ap().rearrange("(r p) n -> p r n", p=128)),
             reads=("ccout",), writes=("gath",), dsem=xs)
        P.op("dve", lambda e: e.tensor_scalar(acc, gath[:, 0, :], onehot[:, 0:1], None, ALU.mult), reads=("gath",), writes=("acc",))
        for r in range(1, 2):
            P.op("dve", lambda e, r=r: e.scalar_tensor_tensor(acc, gath[:, r, :], onehot[:, r:r + 1], acc, ALU.mult, ALU.add),
                 reads=("gath", "acc"), writes=("acc",))
        P.op("dve", lambda e: e.tensor_copy(zall[:, :, 0:HALO], acc[:, 64:192].rearrange("p (a b) -> p a b", a=4)),
             reads=("acc",), writes=("zhalo",))
        P.barrier()
        block_scan(acc[:, 0:64])
        S(lambda e: e.tensor_copy(v4(W2b)[:, 0], Rall[:, 0, :, 0:NB]))
        S(lambda e: e.tensor_copy(v4(W2b)[:, 1], Rall[:, 1, :, 0:NB]))
        S(lambda e: e.tensor_copy(vxs[:, :, 0], Rall[:, 0, :, 0]), r=("scan", "vxs"), w=("scan", "vxs"))
        for j in range(KB):
            S(lambda e: e.tensor_tensor(v3(Lst), v3(CBr), swp(W2b), ALU.mult))
            S(lambda e: e.tensor_tensor(W2b, W2b, CAr, ALU.mult))
            S(lambda e: e.tensor_tensor(W2b, W2b, Lst, ALU.add))
            S(lambda e, j=j: e.tensor_tensor(vxs[:, :, 1 + j:1 + j + (NB - 1) * KB + 1:KB], vxs[:, :, 1 + j:1 + j + (NB - 1) * KB + 1:KB],
                                             v4(W2b)[:, 0], ALU.add), r=("scan", "vxs"), w=("scan", "vxs"))
        P.barrier()
        P.op("pool", lambda e: e.dma_start(out=esel, in_=esel_d.rearrange("p (a b) -> p a b", a=64)),
             writes=("esel",), dsem=esem)
        for g in range(G):
            bk = bank([0, 1, 2, 3, 4, 5])
            mm_group(bk, banks[bk][:, 0:NCH], [(mbf[:, g, :], ug[:, g, :], (("ug", g), "mbf")),
                                               (woutb[:, g, :], vxs[:, g, 0:NCH], ("vxs",))])
            P.op("act", lambda e, g=g, bk=bk: e.activation(ug[:, g, :], banks[bk][:, 0:NCH], AF.Copy),
                 reads=(("ps", bk),), writes=(("ug", g),))
        for q in range(4):
            for t in range(8):
                bk = bank([4, 5, 6, 7])
                mm_group(bk, banks[bk][:, 0:NCH],
                         [(esel[:, t * 8 + gl, :], ug[:, q * 8 + gl, :], (("ug", q * 8 + gl), "esel")) for gl in range(8)])
                ps = banks[bk][:, 0:NCH]
                par = (q * 8 + t) % 2
                a0, a1, a2 = gtmp[par]
                k0, k1, k2 = ("g0", par), ("g1", par), ("g2", par)
                P.op("act", lambda e, ps=ps, a0=a0: e.activation(a0, ps, AF.Square, scale=math.sqrt(0.044715)),
                     reads=(("ps", bk),), writes=(k0,))
                P.op("dve", lambda e, ps=ps, a0=a0, a1=a1: e.scalar_tensor_tensor(a1, a0, 1.0, ps, ALU.add, ALU.mult),
                     reads=(k0, ("ps", bk)), writes=(k1,))
                P.op("act", lambda e, a1=a1, a2=a2: e.activation(a2, a1, AF.Sigmoid, scale=2.0 * math.sqrt(2.0 / math.pi)),
                     reads=(k1,), writes=(k2,))
                P.op("dve", lambda e, ps=ps, q=q, t=t, a2=a2: e.tensor_tensor(ufm[:, q, t::8], a2, ps, ALU.mult),
                     reads=(k2, ("ps", bk)), writes=("yg",))
        P.barrier()
        dump("yg", ufm)
        dump("mbf", mbf)
        dump("wst3", wst3)
        dump("woutb", woutb)
        dump("ug", ug)
        dump("vxs", vxs)
        ar.release(m2)

        ar.size = ARENA
        m5 = ar.mark()
        xn = [ar.bf16(DC, TS) for _ in range(SPT)]
        sq = ar.bf16(DC, TS)
        sil = [ar.f32(TS) for _ in range(2)]
        cacc = [ar.f32(4, TS) for _ in range(SPT)]
        cs = [ar.bf16(4, TS) for _ in range(SPT)]
        cab = ar.bf16(4, TS)
        csq = ar.bf16(4, TS)
        m4 = ar.mark()
        gates = [ar.bf16(16, TS) for _ in range(SPT)]
        mA = ar.mark()
        ar.release(m4)
        hid = [ar.bf16(FC, TS) for _ in range(SPT)]
        ar.off = max(ar.off, mA)
        mrg = [ar.bf16(DC, TS) for _ in range(SPT)]
        ysb = [ar.f32(TS) for _ in range(2)]
        mt = [ar.f32(TS) for _ in range(2)]
        lnm = ar.f32(TS)
        lnv = ar.f32(TS)
        w13 = WStream(P, ar, "v13", 4, DC, 128, group=2)
        w2s = WStream(P, ar, "v2", 2, FC, 128)
        wk8 = WStream(P, ar, "vk8", 4, DC, 128)
        wk4 = WStream(P, ar, "vk4", 6, 4, 128, group=3)
        for st in range(NST):
            wk8.plan(wcols_units(w_in, DC, list(range(12, 28))))
            u4 = []
            for d in range(DC):
                u4 += wcols_units(conv_proj, 4, [d]) + wcols_units(w_v, 4, [d]) + wcols_units(w_g, 4, [d])
            wk4.plan(u4)
            wk8.plan(wcols_units(w_out, DC, list(range(DC))))
            w13.plan(w13_units(1))
            w2s.plan(w2_units(1))
        def conv_thunks(st):
            subs = [st * SPT + i for i in range(SPT)]
            T = []
            for k in range(31):
                for i, sub in enumerate(subs):
                    for q in range(4):
                        zc = slice(sub * TS + 2 + k, sub * TS + 2 + k + TS)
                        wcol = vecs[:, V_DW + q * 31 + k:V_DW + q * 31 + k + 1]
                        if k == 0:
                            T.append(lambda i=i, q=q, zc=zc, wcol=wcol, sub=sub: P.op("dve", lambda e: e.tensor_scalar(
                                cacc[i][:, q, :], zall[:, q, zc], wcol, vecs[:, V_DWB + q:V_DWB + q + 1], ALU.mult, ALU.add),
                                reads=(("z", sub), ("z", sub - 1), "zhalo", ("cs", i)), writes=(("cacc", i, q),)))
                        else:
                            T.append(lambda i=i, q=q, zc=zc, wcol=wcol: P.op("dve", lambda e: e.scalar_tensor_tensor(
                                cacc[i][:, q, :], zall[:, q, zc], wcol, cacc[i][:, q, :], ALU.mult, ALU.add),
                                reads=(("cacc", i, q),), writes=(("cacc", i, q),)))
            for i, sub in enumerate(subs):
                ck = tuple(("cacc", i, q) for q in range(4))
                T.append(lambda i=i, ck=ck: P.op("act", lambda e: e.activation(cab, cacc[i], AF.Copy), reads=ck, writes=("cab",)))
                T.append(lambda i=i, ck=ck: P.op("act", lambda e: e.activation(csq, cacc[i], AF.Square), reads=ck, writes=("csq",)))
                T.append(lambda: mm_group(6, banks[6][:, 0:TS], [(onesb, cab[:, q, :], ("cab",)) for q in range(4)]))
                T.append(lambda: P.op("dve", lambda e: e.tensor_scalar(lnm, banks[6][:, 0:TS], 1.0 / 512, None, ALU.mult),
                                      reads=(("ps", 6),), writes=("lnm",)))
                T.append(lambda: mm_group(6, banks[6][:, 0:TS], [(onesb, csq[:, q, :], ("csq",)) for q in range(4)]))
                T.append(lambda: P.op("dve", lambda e: e.tensor_tensor(lnv, lnm, lnm, ALU.mult), reads=("lnm",), writes=("lnv",)))
                T.append(lambda: P.op("dve", lambda e: e.scalar_tensor_tensor(lnv, banks[6][:, 0:TS], 1.0 / 512, lnv, ALU.mult, ALU.subtract),
                                      reads=(("ps", 6), "lnv"), writes=("lnv",)))
                T.append(lambda: P.op("dve", lambda e: e.tensor_scalar(lnv, lnv, EPS, None, ALU.add), reads=("lnv",), writes=("lnv",)))
                T.append(lambda: P.op("act", lambda e: e.activation(lnv, lnv, AF.Sqrt), reads=("lnv",), writes=("lnv",)))
                T.append(lambda: P.op("dve", lambda e: e.reciprocal(lnv, lnv), reads=("lnv",), writes=("lnv",)))
                for q in range(4):
                    T.append(lambda i=i, q=q: P.op("dve", lambda e: e.tensor_tensor(cacc[i][:, q, :], cacc[i][:, q, :], lnm, ALU.subtract),
                                                   reads=(("cacc", i, q), "lnm"), writes=(("cacc", i, q),)))
                    T.append(lambda i=i, q=q: P.op("dve", lambda e: e.tensor_tensor(cacc[i][:, q, :], cacc[i][:, q, :], lnv, ALU.mult),
                                                   reads=(("cacc", i, q), "lnv"), writes=(("cacc", i, q),)))
                    T.append(lambda i=i, q=q: P.op("act", lambda e: e.activation(cs[i][:, q, :], cacc[i][:, q, :], AF.Silu,
                                                                                 bias=vecs[:, V_LNB + q:V_LNB + q + 1],
                                                                                 scale=vecs[:, V_LNG + q:V_LNG + q + 1]),
                                                   reads=(("cacc", i, q),), writes=(("cs", i),)))
            return T

        cur = conv_thunks(0)
        nxt = []

        def fill(n):
            for _ in range(n):
                if cur:
                    cur.pop(0)()
                elif nxt:
                    nxt.pop(0)()
                else:
                    return

        def drain_cur():
            while cur:
                cur.pop(0)()

        for st in range(NST):
            subs = [st * SPT + i for i in range(SPT)]
            if st + 1 < NST:
                nxt.extend(conv_thunks(st + 1))
            wk8.prefetch()
            wk4.prefetch()
            for i, sub in enumerate(subs):
                rmsnorm_to(sub, V_MIXN, xn[i], ("xn", i), sq)
            for o in range(16):
                wu, ku = wk8.get()
                for i, sub in enumerate(subs):
                    b = bank([0, 1, 2, 3, 4, 5])
                    mm_group(b, banks[b][:, 0:TS], [(wu[:, k, :], xn[i][:, k, :], (ku, ("xn", i))) for k in range(DC)])
                    P.op("act", lambda e, b=b, o=o, i=i: e.activation(gates[i][:, o, :], banks[b][:, 0:TS], AF.Sigmoid,
                                                                     bias=vecs[:, V_BG + o:V_BG + o + 1]),
                         reads=(("ps", b),), writes=(("gates", i), "gh") if o == 0 else (("gates", i),))
                    fill(1)
            drain_cur()
            for d in range(DC):
                wc_, kc_ = wk4.get()
                wv_, kv = wk4.get()
                wg_, kg = wk4.get()
                for i, sub in enumerate(subs):
                    c = cols(sub)
                    bc_ = bank([0, 1, 2, 3, 4, 5])
                    bv = bank([0, 1, 2, 3, 4, 5])
                    bg = bank([0, 1, 2, 3, 4, 5])
                    mm_group(bc_, banks[bc_][:, 0:TS], [(wc_[:, k, :], cs[i][:, k, :], (kc_, ("cs", i))) for k in range(4)])
                    mm_group(bv, banks[bv][:, 0:TS], [(wv_[:, k, :], ufm[:, k, c], (kv, "yg")) for k in range(4)])
                    mm_group(bg, banks[bg][:, 0:TS], [(wg_[:, k, :], ufm[:, k, c], (kg, "yg")) for k in range(4)])
                    j = (d * SPT + i) % 2
                    P.op("act", lambda e, j=j, bg=bg: e.activation(ysb[j], banks[bg][:, 0:TS], AF.Sigmoid),
                         reads=(("ps", bg),), writes=(("ysb", j),))
                    P.op("dve", lambda e, j=j, bv=bv: e.tensor_tensor(ysb[j], ysb[j], banks[bv][:, 0:TS], ALU.mult),
                         reads=(("ysb", j), ("ps", bv)), writes=(("ysb", j),))
                    P.op("dve", lambda e, j=j, i=i, d=d: e.tensor_tensor(ysb[j], ysb[j], gates[i][:, 8 + d, :], ALU.mult),
                         reads=(("ysb", j), ("gates", i), "gh"), writes=(("ysb", j),))
                    P.op("dve", lambda e, j=j, i=i, d=d, bc_=bc_: e.tensor_tensor(mt[j], gates[i][:, d, :], banks[bc_][:, 0:TS], ALU.mult),
                         reads=(("gates", i), "gh", ("ps", bc_)), writes=(("mt", j),))
                    P.op("dve", lambda e, j=j, i=i, d=d: e.tensor_tensor(mrg[i][:, d, :], mt[j], ysb[j], ALU.add),
                         reads=(("mt", j), ("ysb", j)), writes=(("mrg", i),))
            w13.prefetch()
            for d in range(DC):
                wo, ko = wk8.get()
                for i, sub in enumerate(subs):
                    c = cols(sub)
                    bo = bank([0, 1, 2, 3])
                    mm_group(bo, banks[bo][:, 0:TS], [(wo[:, k, :], mrg[i][:, k, :], (ko, ("mrg", i))) for k in range(DC)])
                    P.op("dve", lambda e, bo=bo, d=d, c=c: e.tensor_tensor(hres[:, d, c], hres[:, d, c], banks[bo][:, 0:TS], ALU.add),
                         reads=(("ps", bo), hkey(sub)), writes=(hkey(sub),))
                    fill(1)
            ffn(st, 1, xn, hid, sq, sil, w13, w2s, fill=fill)
            cur.extend(nxt)
            del nxt[:]
        P.barrier()
        ar.release(m5)

        fsq = [ar.bf16(DC, 128) for _ in range(2)]
        frs = [ar.f32(128) for _ in range(2)]
        fon = [ar.f32(DC, 128) for _ in range(2)]
        oblk = [ar.f32(D) for _ in range(2)]
        osem = [P.dsem("o") for _ in range(2)]

        def fin_a(bi):
            r0 = bi * 128
            nr = min(128, NT - r0)
            s = bi % 2
            c = slice(r0, r0 + nr)
            sb = 4 + s
            P.op("act", lambda e: e.activation(fsq[s][:, :, 0:nr], hres[:, :, c], AF.Square), writes=(("fsq", s),))
            mm_group(sb, banks[sb][:, 0:nr], [(onesb, fsq[s][:, k, 0:nr], (("fsq", s),)) for k in range(DC)])
            P.op("dve", lambda e: e.tensor_scalar(frs[s][:, 0:nr], banks[sb][:, 0:nr], 1.0 / D, EPS, ALU.mult, ALU.add),
                 reads=(("ps", sb),), writes=(("frs", s),))
            P.op("act", lambda e: e.activation(frs[s][:, 0:nr], frs[s][:, 0:nr], AF.Sqrt), reads=(("frs", s),), writes=(("frs", s),))
            P.op("dve", lambda e: e.reciprocal(frs[s][:, 0:nr], frs[s][:, 0:nr]), reads=(("frs", s),), writes=(("frs", s),))
            for k in range(DC):
                P.op("dve", lambda e, k=k: e.scalar_tensor_tensor(
                    fon[s][:, k, 0:nr], hres[:, k, c], vecs[:, V_FIN + k:V_FIN + k + 1], frs[s][:, 0:nr], ALU.mult, ALU.mult),
                    reads=(("frs", s),), writes=(("fon", s),))

        def fin_b(bi):
            r0 = bi * 128
            nr = min(128, NT - r0)
            s = bi % 2
            for hb in range(2):
                bk = 2 * s + hb
                for kk in range(4):
                    k = hb * 4 + kk
                    P.op("pe", lambda e, bk=bk, kk=kk, k=k: e.transpose(banks[bk][0:nr, kk * 128:(kk + 1) * 128], fon[s][:, k, 0:nr], ident),
                         reads=(("fon", s),), writes=(("ps", bk),) if kk in (0, 3) else (), signal=(kk == 3))
                P.op("act", lambda e, bk=bk, hb=hb: e.activation(oblk[s][0:nr, hb * 512:(hb + 1) * 512], banks[bk][0:nr, :], AF.Copy),
                     reads=(("ps", bk),), writes=(("oblk", s),))
            P.op("sp", lambda e: e.dma_start(out=out_d[r0:r0 + nr, :], in_=oblk[s][0:nr, :]),
                 reads=(("oblk", s),), dsem=osem[s])

        fin_a(0)
        for bi in range(nblk):
            if bi + 1 < nblk:
                fin_a(bi + 1)
            fin_b(bi)
        P.barrier()
        P.emit()
    return nc


NVEC = 60 + 124
NSP = 136 + 2048

_CACHE = {}
_DBG = False
_DBG_HOOK = None


def _consts():
    ident = np.eye(128, dtype=np.float32)
    r = np.arange(128)
    kb, t = r[:, None] // 16, r[None, :] // 16
    mask = ((kb + t) >= 7).astype(np.float32)
    adi = (((kb + t) == 7) & ((r[:, None] % 16) == (r[None, :] % 16))).astype(np.float32)
    cst = np.concatenate([ident, mask, adi, np.zeros((128, 128), np.float32)], axis=1)
    esel = np.zeros((128, 64, 128), np.float32)
    for a in range(8):
        for b in range(8):
            for h in range(16):
                esel[a * 16 + h, a * 8 + b, b * 16 + h] = 1.0
    return np.ascontiguousarray(cst), np.ascontiguousarray(esel.reshape(128, 64 * 128))


def kernel(x, meta_tokens, ffn1_norm, ffn1_w1, ffn1_w3, ffn1_w2, mix_norm, w_in, b_gate,
           conv_dw, conv_dw_b, conv_ln_g, conv_ln_b, conv_proj,
           ssm_lam_re, ssm_lam_im, ssm_log_dt, ssm_b_re, ssm_b_im, ssm_c_re, ssm_c_im,
           ssm_d, ssm_w_v, ssm_w_g, w_out, ffn2_norm, ffn2_w1, ffn2_w3, ffn2_w2, final_norm):
    f = lambda a: np.ascontiguousarray(np.asarray(a, dtype=np.float32))
    x = f(x)
    B = x.shape[0]
    def colv(v, nchunk):
        return f(v).reshape(nchunk, 128).T
    vecs = np.concatenate([
        colv(ffn1_norm[0], 8), colv(mix_norm[0], 8), colv(ffn2_norm[0], 8), colv(final_norm, 8),
        colv(b_gate[0], 16), colv(conv_dw_b[0], 4), colv(conv_ln_g[0], 4), colv(conv_ln_b[0], 4),
        f(conv_dw[0]).T.reshape(4, 128, 31).transpose(1, 0, 2).reshape(128, 124),
    ], axis=1)
    assert vecs.shape == (128, NVEC)
    dup = lambda a: np.concatenate([a, a], axis=0)
    lre = dup(f(ssm_lam_re[0]).T)
    lim = dup(f(ssm_lam_im[0]).T)
    ldt = np.broadcast_to(f(ssm_log_dt[0])[None, :], (128, G))
    dcol = np.tile(f(ssm_d[0]).reshape(G, 16).T, (8, 1))
    bre = dup(f(ssm_b_re[0]).transpose(1, 0, 2).reshape(64, G * 16))
    bim = dup(f(ssm_b_im[0]).transpose(1, 0, 2).reshape(64, G * 16))
    cre = dup(f(ssm_c_re[0]).transpose(2, 0, 1).reshape(64, G * 16))
    cim = dup(f(ssm_c_im[0]).transpose(2, 0, 1).reshape(64, G * 16))
    cst, esel = _consts()

    def tile_w(w):
        w = f(w)
        K, N = w.shape
        return np.ascontiguousarray(w.reshape(K // 128, 128, N // 128, 128).transpose(2, 1, 0, 3).reshape(N // 128, 128, K))
    wt = {"ffn1_w1": tile_w(ffn1_w1[0]), "ffn1_w3": tile_w(ffn1_w3[0]), "ffn1_w2": tile_w(ffn1_w2[0]),
          "ffn2_w1": tile_w(ffn2_w1[0]), "ffn2_w3": tile_w(ffn2_w3[0]), "ffn2_w2": tile_w(ffn2_w2[0]),
          "w_in": tile_w(w_in[0]), "conv_proj": tile_w(conv_proj[0]), "ssm_w_v": tile_w(ssm_w_v[0]),
          "ssm_w_g": tile_w(ssm_w_g[0]), "w_out": tile_w(w_out[0])}
    meta = f(meta_tokens)
    half = NT - 16
    in_maps = []
    for core in range(NCORES):
        b, s = core // 2, core % 2
        if s == 0:
            xin = np.concatenate([meta, x[b, 0:half]], axis=0)
        else:
            xin = x[b, half:]
        oh = np.zeros((128, 8), np.float32)
        if s == 1:
            oh[:, 0] = 1.0
        sp = np.concatenate([lre, lim, ldt, dcol, oh, bre, bim, cre, cim], axis=1)
        assert sp.shape == (128, NSP)
        in_maps.append({
            "xin": f(xin), "vecs": f(vecs), "sp": f(sp), "cst": cst, "esel": esel, **wt,
        })
    if "nc" not in _CACHE:
        nc = bass.Bass("TRN2", target_bir_lowering=False)
        _CACHE["nc"] = build(nc)
    res = run_bass_kernel_spmd(_CACHE["nc"], in_maps, core_ids=list(range(NCORES)))
    if _DBG_HOOK is not None:
        _DBG_HOOK(res)
    out = np.empty((B, 4096, D), np.float32)
    for core in range(NCORES):
        b, s = core // 2, core % 2
        o = np.asarray(res.results[core]["out"], dtype=np.float32)
        if s == 0:
            out[b, 0:half] = o[16:]
        else:
            out[b, half:] = o
    return out
```

```python
import math
from contextlib import ExitStack

import numpy as np
import concourse.bass as bass
import concourse.mybir as mybir
from concourse.bass_utils import run_bass_kernel_spmd

F32 = mybir.dt.float32
BF16 = mybir.dt.bfloat16
AF = mybir.ActivationFunctionType
ALU = mybir.AluOpType

NCORES = 8
NT = 2056
TS = 257
NSUB = 8
SPT = 2
NST = NSUB // SPT
D = 1024
DC = 8
DFF = 2816
FC = 22
NCH = 257
G = 32
HALO = 32
EPS = 1e-6
ARENA = 53200


class Tok:
    __slots__ = ("sem", "val", "eng")

    def __init__(self, sem, val, eng):
        self.sem, self.val, self.eng = sem, val, eng


class Prog:
    def __init__(self, nc, es):
        self.nc, self.es = nc, es
        self.engs = {"pe": nc.tensor, "act": nc.scalar, "dve": nc.vector, "pool": nc.gpsimd, "sp": nc.sync}
        self.ops = {e: [] for e in self.engs}
        self.cur = {}
        self.waited = {e: {} for e in self.engs}
        self.lastw = {}
        self.readers = {}
        self.pend_r, self.pend_w = [], []
        self.nsem = 0
        self.dsems = []

    def newsem(self, name):
        self.nsem += 1
        return self.es.enter_context(self.nc.semaphore(f"{name}{self.nsem}"))

    def dsem(self, name="d"):
        d = [self.newsem(name), 0]
        self.dsems.append(d)
        return d

    def engsem(self, e):
        c = self.cur.get(e)
        if c is None or c[1] >= 30000:
            c = [self.newsem("s" + e), 0]
            self.cur[e] = c
        return c

    def _need(self, e, toks):
        need = {}
        for t in toks:
            if e == "pe" and t.eng == "pe":
                continue
            sid = id(t.sem)
            if t.val <= self.waited[e].get(sid, 0):
                continue
            if sid not in need or need[sid].val < t.val:
                need[sid] = t
        for sid, t in need.items():
            self.waited[e][sid] = t.val
        return list(need.values())

    def op(self, e, fn, reads=(), writes=(), signal=True, dsem=None, inc=1):
        toks = []
        for k in reads:
            t = self.lastw.get(k)
            if t is not None:
                toks.append(t)
        for k in writes:
            t = self.lastw.get(k)
            if t is not None:
                toks.append(t)
            toks.extend(self.readers.get(k, {}).values())
        waits = self._need(e, toks)
        tok, sig = None, None
        if dsem is not None:
            dsem[1] += 16
            tok = Tok(dsem[0], dsem[1], "dma")
            sig = (dsem[0], 16)
        elif signal:
            c = self.engsem(e)
            c[1] += inc
            tok = Tok(c[0], c[1], e)
            sig = (c[0], inc)
        self.ops[e].append((waits, fn, sig))
        if e == "pe" and tok is None:
            self.pend_r.extend(reads)
            self.pend_w.extend(writes)
            return None
        if e == "pe":
            reads = list(reads) + self.pend_r
            writes = list(writes) + self.pend_w
            self.pend_r, self.pend_w = [], []
        for k in writes:
            self.lastw[k] = tok
            self.readers[k] = {}
        for k in reads:
            r = self.readers.setdefault(k, {})
            sid = id(tok.sem)
            if sid not in r or r[sid].val < tok.val:
                r[sid] = tok
        return tok

    def barrier(self):
        toks = [Tok(c[0], c[1], e) for e, c in self.cur.items() if c[1] > 0]
        toks += [Tok(d[0], d[1], "dma") for d in self.dsems if d[1] > 0]
        for e in self.engs:
            w = []
            for t in toks:
                sid = id(t.sem)
                if t.val > self.waited[e].get(sid, 0):
                    self.waited[e][sid] = t.val
                    w.append(t)
            if w:
                self.ops[e].append((w, None, None))

    def emit(self):
        with self.nc.Block() as block:
            def mk(e):
                def f(eng):
                    for waits, fn, sig in self.ops[e]:
                        for t in waits:
                            eng.wait_ge(t.sem, t.val)
                        if fn is None:
                            continue
                        ins = fn(eng)
                        if sig is not None:
                            ins.then_inc(sig[0], sig[1])
                return f
            block.tensor(mk("pe"))
            block.scalar(mk("act"))
            block.vector(mk("dve"))
            block.gpsimd(mk("pool"))
            block.sync(mk("sp"))


class Arena:
    def __init__(self, t, size, base=0):
        self.t, self.size, self.off = t, size, base

    def mark(self):
        return self.off

    def release(self, m):
        self.off = m

    def f32(self, *shape):
        n = int(np.prod(shape))
        assert self.off + n <= self.size, f"arena overflow {self.off}+{n}"
        ap = self.t[:, self.off:self.off + n]
        self.off += n
        return _shape(ap, shape)

    def bf16(self, *shape):
        n = int(np.prod(shape))
        nf = (n + 1) // 2
        assert self.off + nf <= self.size, f"arena overflow {self.off}+{nf}"
        ap = self.t[:, self.off:self.off + nf].bitcast(BF16)[:, 0:n]
        self.off += nf
        return _shape(ap, shape)


def _shape(ap, shape):
    if len(shape) == 1:
        return ap
    if len(shape) == 2:
        return ap.rearrange("p (a b) -> p a b", a=shape[0])
    if len(shape) == 3:
        return ap.rearrange("p (a b c) -> p a b c", a=shape[0], b=shape[1])
    raise ValueError(shape)


class WStream:
    def __init__(self, P, ar, name, nslots, kc, ncol, group=1):
        self.P, self.name, self.n, self.group = P, name, nslots, group
        self.slots = [ar.bf16(kc, ncol) for _ in range(nslots)]
        self.sems = [P.dsem("w" + name) for _ in range(nslots)]
        self.units, self.issued, self.consumed = [], 0, 0

    def plan(self, units):
        self.units.extend(units)

    def _issue(self):
        i = self.issued
        s = i % self.n
        src, kc, ncol = self.units[i]
        dst = self.slots[s][:, 0:kc, 0:ncol].rearrange("p a b -> p (a b)")
        if kc * ncol > 2048:
            dst = dst.rearrange("p (a b) -> p a b", a=2)
            src = src.rearrange("p (a b) -> p a b", a=2)
        self.P.op("pool", lambda e, d=dst, r=src: e.dma_start(out=d, in_=r),
                  writes=((self.name, s),), dsem=self.sems[s])
        self.issued += 1

    def prefetch(self):
        while self.issued < len(self.units) and self.issued < self.consumed + self.n - (self.group - 1):
            self._issue()

    def get(self):
        self.prefetch()
        i = self.consumed
        s = i % self.n
        _, kc, ncol = self.units[i]
        self.consumed += 1
        return self.slots[s][:, 0:kc, 0:ncol], (self.name, s)


def build(nc):
    dram = {}

    def din(name, shape):
        dram[name] = nc.dram_tensor(name, list(shape), F32, kind="ExternalInput").ap()
        return dram[name]

    xin = din("xin", (NT, D))
    w1a = [din(f"ffn{i}_w1", (FC, 128, DC * 128)) for i in (1, 2)]
    w3a = [din(f"ffn{i}_w3", (FC, 128, DC * 128)) for i in (1, 2)]
    w2a = [din(f"ffn{i}_w2", (DC, 128, FC * 128)) for i in (1, 2)]
    w_in = din("w_in", (28, 128, DC * 128))
    conv_proj = din("conv_proj", (DC, 128, 4 * 128))
    w_v = din("ssm_w_v", (DC, 128, 4 * 128))
    w_g = din("ssm_w_g", (DC, 128, 4 * 128))
    w_out = din("w_out", (DC, 128, DC * 128))
    vecs_d = din("vecs", (128, NVEC))
    sp_d = din("sp", (128, NSP))
    cst_d = din("cst", (128, 4 * 128))
    esel_d = din("esel", (128, 64 * 128))
    out_d = nc.dram_tensor("out", [NT, D], F32, kind="ExternalOutput").ap()
    cc_in = nc.dram_tensor("cc_in", [128, 192], F32)
    cc_out = nc.dram_tensor("cc_out", [2 * 128, 192], F32)

    es = ExitStack()
    with es:
        es.enter_context(nc.allow_low_precision("bf16 matmul operands, fp32 accumulation"))
        es.enter_context(nc.allow_non_contiguous_dma("layout"))
        arena_t = es.enter_context(nc.sbuf_tensor("arena", [128, ARENA], F32))
        banks = [es.enter_context(nc.psum_tensor(f"ps{i}", [128, 512], F32)) for i in range(8)]
        P = Prog(nc, es)
        ar = Arena(arena_t, ARENA)

        def wr(x):
            return dram[x] if isinstance(x, str) else x

        hres = ar.f32(DC, NT)
        ufm = ar.bf16(4, NT)
        zall = ar.bf16(4, HALO + NT)
        vecs = ar.f32(NVEC)
        ident = ar.f32(128)
        onesb = ar.bf16(128)
        rstd = ar.f32(TS)

        csem = P.dsem("c")
        DBG = _DBG
        dbg = {}
        dbgsem = P.dsem("g")

        def dump(name, ap3):
            if not DBG:
                return
            shp = list(ap3.shape)
            dbg[name] = nc.dram_tensor("dbg_" + name, shp, F32, kind="ExternalOutput").ap()
            P.barrier()
            P.op("pool", lambda e: e.dma_start(out=dbg[name], in_=ap3), dsem=dbgsem)
            P.barrier()
        P.op("sp", lambda e: e.dma_start(out=vecs, in_=vecs_d), writes=("vecs",), dsem=csem)
        P.op("sp", lambda e: e.dma_start(out=ident, in_=cst_d[:, 0:128]), writes=("ident",), dsem=csem)
        P.op("dve", lambda e: e.memset(onesb, 1.0), writes=("ones",))
        P.op("dve", lambda e: e.memset(zall[:, :, 0:HALO], 0.0), writes=("zhalo",))
        P.barrier()

        V_F1N, V_MIXN, V_F2N, V_FIN, V_BG, V_DWB, V_LNG, V_LNB, V_DW = 0, 8, 16, 24, 32, 48, 52, 56, 60

        psrr = [0]

        def bank(pool):
            b = pool[psrr[0] % len(pool)]
            psrr[0] += 1
            return b

        def mm_group(bk, out_ap, terms, extra_w=()):
            n = len(terms)
            for i, (lhsT, rhs, rk) in enumerate(terms):
                last = i == n - 1
                P.op("pe", lambda e, o=out_ap, l=lhsT, r=rhs, st=(i == 0), sp_=last:
                     e.matmul(o, l, r, start=st, stop=sp_),
                     reads=rk, writes=(("ps", bk),) if (i == 0 or last) else (), signal=last)

        TOPN = 7500
        top = Arena(arena_t, ARENA, base=ARENA - TOPN)
        ar.size = ARENA - TOPN
        wst3 = top.bf16(G, 192)
        mbf = top.bf16(G, 128)
        woutb = top.bf16(G, 128)
        CA16 = top.f32(64)
        CB16 = top.f32(64)
        CA = top.f32(64)
        CB = top.f32(64)
        onehot = top.f32(8)
        dcol = top.f32(G)
        m0 = ar.mark()
        xblk = [ar.f32(D) for _ in range(2)]
        xsem = [P.dsem("x") for _ in range(2)]
        nblk = (NT + 127) // 128
        def in_block(bi):
            r0 = bi * 128
            nr = min(128, NT - r0)
            s = bi % 2
            P.op("sp", lambda e, s=s, r0=r0, nr=nr: e.dma_start(out=xblk[s][0:nr, :], in_=xin[r0:r0 + nr, :]),
                 writes=(("xblk", s),), dsem=xsem[s])
            for hb in range(2):
                bk = 6 + hb
                for kk in range(4):
                    k = hb * 4 + kk
                    P.op("pe", lambda e, bk=bk, kk=kk, k=k, s=s, nr=nr:
                         e.transpose(banks[bk][:, kk * 128:kk * 128 + nr], xblk[s][0:nr, k * 128:(k + 1) * 128],
                                     ident[0:nr, 0:nr]),
                         reads=(("xblk", s),), writes=(("ps", bk),) if kk in (0, 3) else (), signal=(kk == 3))
                eng = "act" if hb == 0 else "dve"
                src = banks[bk][:, :].rearrange("p (a b) -> p a b", a=4)[:, :, 0:nr]
                dst = hres[:, hb * 4:hb * 4 + 4, r0:r0 + nr]
                if eng == "act":
                    P.op("act", lambda e, d=dst, s_=src: e.activation(d, s_, AF.Copy),
                         reads=(("ps", bk),), writes=(("hres", bi),))
                else:
                    P.op("dve", lambda e, d=dst, s_=src: e.tensor_copy(d, s_),
                         reads=(("ps", bk),), writes=(("hres", bi),))
        inq = list(range(nblk))

        def in_fill():
            if inq:
                in_block(inq.pop(0))

        in_fill()
        in_fill()
        mask = ar.f32(128)
        adi = ar.f32(128)
        spt = ar.f32(NSP)
        ssem = P.dsem("s")
        P.op("sp", lambda e: e.dma_start(out=mask, in_=cst_d[:, 128:256]), writes=("prep",), dsem=ssem)
        P.op("sp", lambda e: e.dma_start(out=adi, in_=cst_d[:, 256:384]), writes=("prep",), dsem=ssem)
        P.op("sp", lambda e: e.dma_start(out=spt, in_=sp_d), writes=("prep",), dsem=ssem)
        P.op("dve", lambda e: e.tensor_copy(dcol, spt[:, 96:128]), reads=("prep",), writes=("prep",))
        P.op("dve", lambda e: e.tensor_copy(onehot, spt[:, 128:136]), reads=("prep",), writes=("prep",))
        lre, lim, ldt = spt[:, 0:32], spt[:, 32:64], spt[:, 64:96]
        o0 = 136
        Bre = spt[:, o0:o0 + 512].rearrange("p (g h) -> p g h", g=G)
        Bim = spt[:, o0 + 512:o0 + 1024].rearrange("p (g h) -> p g h", g=G)
        Cre = spt[:, o0 + 1024:o0 + 1536].rearrange("p (g h) -> p g h", g=G)
        Cim = spt[:, o0 + 1536:o0 + 2048].rearrange("p (g h) -> p g h", g=G)

        pcnt = [0]

        def tick():
            pcnt[0] += 1
            if pcnt[0] % 14 == 0:
                in_fill()

        def V(fn):
            P.op("dve", fn, reads=("prep",), writes=("prep",))
            tick()

        def A(fn):
            P.op("act", fn, reads=("prep",), writes=("prep",))
            tick()

        def tt(o, a, b, op):
            V(lambda e: e.tensor_tensor(o, a, b, op))

        def ts(o, a, s1, s2, op0, op1=None):
            if op1 is None:
                V(lambda e: e.tensor_scalar(o, a, s1, None, op0))
            else:
                V(lambda e: e.tensor_scalar(o, a, s1, s2, op0, op1))

        def tmpg(n=1):
            return [ar.f32(G) for _ in range(n)]

        dt_, ar_, ai_, mag, imag, s1, c1, wtmp = tmpg(8)
        A(lambda e: e.activation(dt_, ldt, AF.Exp))
        tt(ar_, lre, dt_, ALU.mult)
        tt(ai_, lim, dt_, ALU.mult)
        A(lambda e: e.activation(mag, ar_, AF.Exp))
        A(lambda e: e.activation(imag, ar_, AF.Exp, scale=-1.0))
        TWO_PI = 2.0 * math.pi
        MAGIC = 12582912.0
        wn, = tmpg(1)

        def sin_of(o, x, shift):
            ts(wtmp, x, shift, None, ALU.add)
            ts(wn, wtmp, 1.0 / TWO_PI, MAGIC, ALU.mult, ALU.add)
            ts(wn, wn, -MAGIC, None, ALU.add)
            V(lambda e: e.scalar_tensor_tensor(wtmp, wn, -TWO_PI, wtmp, ALU.mult, ALU.add))
            ts(wn, wtmp, math.pi, TWO_PI, ALU.is_gt, ALU.mult)
            tt(wtmp, wtmp, wn, ALU.subtract)
            ts(wn, wtmp, -math.pi, TWO_PI, ALU.is_lt, ALU.mult)
            tt(wtmp, wtmp, wn, ALU.add)
            A(lambda e: e.activation(o, wtmp, AF.Sin))

        sin_of(s1, ai_, 0.0)
        sin_of(c1, ai_, 0.5 * math.pi)
        L1r, L1i, N1r, N1i = tmpg(4)
        tt(L1r, mag, c1, ALU.mult)
        tt(L1i, mag, s1, ALU.mult)
        tt(N1r, imag, c1, ALU.mult)
        V(lambda e: e.scalar_tensor_tensor(N1i, imag, -1.0, s1, ALU.mult, ALU.mult))
        PWr = ar.f32(16, G)
        PWi = ar.f32(16, G)
        V(lambda e: e.memset(PWr[:, 7, :], 1.0))
        V(lambda e: e.memset(PWi[:, 7, :], 0.0))
        t1, t2 = tmpg(2)

        def cmul(orr, oi, ar0, ai0, br0, bi0):
            tt(t1, ar0, br0, ALU.mult)
            tt(t2, ai0, bi0, ALU.mult)
            tt(orr, t1, t2, ALU.subtract)
            tt(t1, ar0, bi0, ALU.mult)
            tt(t2, ai0, br0, ALU.mult)
            tt(oi, t1, t2, ALU.add)

        for k in range(1, 9):
            cmul(PWr[:, 7 + k, :], PWi[:, 7 + k, :], PWr[:, 6 + k, :], PWi[:, 6 + k, :], L1r, L1i)
        for k in range(1, 8):
            cmul(PWr[:, 7 - k, :], PWi[:, 7 - k, :], PWr[:, 8 - k, :], PWi[:, 8 - k, :], N1r, N1i)
        nr_, den, cr, ci = tmpg(4)
        ts(nr_, L1r, -1.0, None, ALU.add)
        tt(t1, lre, lre, ALU.mult)
        tt(t2, lim, lim, ALU.mult)
        tt(den, t1, t2, ALU.add)
        V(lambda e: e.reciprocal(den, den))
        tt(t1, nr_, lre, ALU.mult)
        tt(t2, L1i, lim, ALU.mult)
        tt(cr, t1, t2, ALU.add)
        tt(cr, cr, den, ALU.mult)
        tt(t1, L1i, lre, ALU.mult)
        tt(t2, nr_, lim, ALU.mult)
        tt(ci, t1, t2, ALU.subtract)
        tt(ci, ci, den, ALU.mult)
        Bbr = ar.f32(G, 16)
        Bbi = ar.f32(G, 16)
        T1 = ar.f32(G, 16)
        T2 = ar.f32(G, 16)

        def bc(x):
            return x.unsqueeze(2).broadcast_to([128, G, 16])

        tt(T1, Bre, bc(cr), ALU.mult)
        tt(T2, Bim, bc(ci), ALU.mult)
        tt(Bbr, T1, T2, ALU.subtract)
        tt(T1, Bim, bc(cr), ALU.mult)
        tt(T2, Bre, bc(ci), ALU.mult)
        tt(Bbi, T1, T2, ALU.add)
        UA = ar.f32(8, G)
        UB = ar.f32(8, G)
        lo, hi = slice(0, 64), slice(64, 128)

        def neg(o, a):
            ts(o, a, -1.0, None, ALU.mult)

        def cp(o, a):
            V(lambda e: e.tensor_copy(o, a))

        cp(UA[lo], PWr[lo, 7:15, :])
        cp(UA[hi], PWi[hi, 7:15, :])
        neg(UB[lo], PWi[lo, 7:15, :])
        cp(UB[hi], PWr[hi, 7:15, :])
        BL = ar.f32(G, 8, 16)
        for kb in range(8):
            tt(T1, Bbr, bc(UA[:, kb, :]), ALU.mult)
            tt(T2, Bbi, bc(UB[:, kb, :]), ALU.mult)
            tt(BL[:, :, kb, :], T1, T2, ALU.add)
        WA = ar.f32(8, G)
        WB = ar.f32(8, G)
        CL = ar.f32(G, 8, 16)

        def build_cl(i0):
            cp(WA[lo], PWr[lo, i0:i0 + 8, :])
            neg(WA[hi], PWi[hi, i0:i0 + 8, :])
            neg(WB[lo], PWi[lo, i0:i0 + 8, :])
            neg(WB[hi], PWr[hi, i0:i0 + 8, :])
            for t in range(8):
                tt(T1, Cre, bc(WA[:, t, :]), ALU.mult)
                tt(T2, Cim, bc(WB[:, t, :]), ALU.mult)
                tt(CL[:, :, t, :], T1, T2, ALU.add)

        build_cl(8)
        cp(woutb, CL.rearrange("p g t h -> p g (t h)"))
        build_cl(0)
        a8, b8 = PWr[:, 15, :], PWi[:, 15, :]
        cp(CA[:, 0:32], a8)
        cp(CA[:, 32:64], a8)
        neg(CB[lo, 0:32], b8[lo])
        cp(CB[hi, 0:32], b8[hi])
        cp(CB[lo, 32:64], b8[lo])
        neg(CB[hi, 32:64], b8[hi])
        q_r, q_i, q_r2, q_i2 = tmpg(4)
        cp(q_r, a8)
        cp(q_i, b8)
        for _ in range(4):
            cmul(q_r2, q_i2, q_r, q_i, q_r, q_i)
            cp(q_r, q_r2)
            cp(q_i, q_i2)
        cp(CA16[:, 0:32], q_r)
        cp(CA16[:, 32:64], q_r)
        neg(CB16[lo, 0:32], q_i[lo])
        cp(CB16[hi, 0:32], q_i[hi])
        cp(CB16[lo, 32:64], q_i[lo])
        neg(CB16[hi, 32:64], q_i[hi])
        BLf = BL.rearrange("p g k h -> p g (k h)")
        CLf = CL.rearrange("p g t h -> p g (t h)")
        mtmp = ar.f32(128)
        for g in range(G):
            bk = 4 + g % 2
            P.op("pe", lambda e, g=g, bk=bk: e.matmul(banks[bk][:, 0:128], BLf[:, g, :], CLf[:, g, :], start=True, stop=True),
                 reads=("prep",), writes=(("ps", bk),))
            P.op("dve", lambda e, bk=bk: e.tensor_tensor(mtmp, banks[bk][:, 0:128], mask, ALU.mult),
                 reads=(("ps", bk),), writes=("mtmp",))
            P.op("dve", lambda e, g=g: e.scalar_tensor_tensor(mbf[:, g, :], adi, dcol[:, g:g + 1], mtmp, ALU.mult, ALU.add),
                 reads=("mtmp",), writes=("mbf",))
            bt = 6 + g % 2
            P.op("pe", lambda e, g=g, bt=bt: e.transpose(banks[bt][:, 0:128], BLf[:, g, :], ident),
                 reads=("prep",), writes=(("ps", bt),))
            P.op("act", lambda e, g=g, bt=bt: e.activation(wst3[:, g, 0:128], banks[bt][:, 0:128], AF.Copy),
                 reads=(("ps", bt),), writes=("wst3",))
            P.op("act", lambda e, g=g, bt=bt: e.activation(wst3[:, g, 128:192], banks[bt][:, 0:64], AF.Copy),
                 reads=(("ps", bt),), writes=("wst3",))
        while inq:
            in_fill()
        P.barrier()
        ar.release(m0)
        dump("h0", hres)

        def hkey(sub):
            return ("h", sub)

        def cols(sub):
            return slice(sub * TS, (sub + 1) * TS)

        def rmsnorm_to(sub, gcol, xn, xnkey, sq):
            c = cols(sub)
            P.op("act", lambda e: e.activation(sq, hres[:, :, c], AF.Square), reads=(hkey(sub),), writes=("sq",))
            bk = 7
            mm_group(bk, banks[bk][:, 0:TS], [(onesb, sq[:, k, :], ("sq",)) for k in range(DC)])
            P.op("dve", lambda e: e.tensor_scalar(rstd, banks[bk][:, 0:TS], 1.0 / D, EPS, ALU.mult, ALU.add),
                 reads=(("ps", bk),), writes=("rstd",))
            P.op("act", lambda e: e.activation(rstd, rstd, AF.Sqrt), reads=("rstd",), writes=("rstd",))
            P.op("dve", lambda e: e.reciprocal(rstd, rstd), reads=("rstd",), writes=("rstd",))
            for k in range(DC):
                P.op("dve", lambda e, k=k: e.scalar_tensor_tensor(xn[:, k, :], hres[:, k, c], vecs[:, gcol + k:gcol + k + 1],
                                                                  rstd, ALU.mult, ALU.mult),
                     reads=(hkey(sub), "rstd"), writes=(xnkey,))

        def ffn(st, which, xn, hid, sq, sil, w13, w2s, fill=None):
            gcol = V_F1N if which == 0 else V_F2N
            subs = [st * SPT + i for i in range(SPT)]
            for i, sub in enumerate(subs):
                rmsnorm_to(sub, gcol, xn[i], ("xn", i), sq)
            w2s.prefetch()
            for f in range(FC):
                wa, ka = w13.get()
                wb, kb = w13.get()
                for i, sub in enumerate(subs):
                    b1 = bank([0, 1, 2, 3, 4, 5])
                    b3 = bank([0, 1, 2, 3, 4, 5])
                    mm_group(b1, banks[b1][:, 0:TS], [(wa[:, k, :], xn[i][:, k, :], (ka, ("xn", i))) for k in range(DC)])
                    mm_group(b3, banks[b3][:, 0:TS], [(wb[:, k, :], xn[i][:, k, :], (kb, ("xn", i))) for k in range(DC)])
                    sl = sil[(f * SPT + i) % 2]
                    skey = ("sil", (f * SPT + i) % 2)
                    P.op("act", lambda e, sl=sl, b1=b1: e.activation(sl, banks[b1][:, 0:TS], AF.Silu),
                         reads=(("ps", b1),), writes=(skey,))
                    P.op("dve", lambda e, sl=sl, b3=b3, i=i, f=f: e.tensor_tensor(hid[i][:, f, :], sl, banks[b3][:, 0:TS], ALU.mult),
                         reads=(skey, ("ps", b3)), writes=(("hid", i), "gh") if f == 0 else (("hid", i),))
                    if fill is not None:
                        fill(3)
            for d in range(DC):
                if fill is not None:
                    fill(8)
                wc, kc_ = w2s.get()
                for i, sub in enumerate(subs):
                    bo = bank([0, 1, 2, 3])
                    mm_group(bo, banks[bo][:, 0:TS], [(wc[:, f, :], hid[i][:, f, :], (kc_, ("hid", i), "gh")) for f in range(FC)])
                    c = cols(sub)
                    P.op("dve", lambda e, bo=bo, d=d, c=c: e.scalar_tensor_tensor(hres[:, d, c], banks[bo][:, 0:TS], 0.5,
                                                                                  hres[:, d, c], ALU.mult, ALU.add),
                         reads=(("ps", bo), hkey(sub)), writes=(hkey(sub),))

        def w13_units(which):
            u = []
            for f in range(FC):
                u.append((w1a[which][f], DC, 128))
                u.append((w3a[which][f], DC, 128))
            return u

        def w2_units(which):
            return [(w2a[which][d], FC, 128) for d in range(DC)]

        def wcols_units(w, kc, colchunks):
            return [(w[o], kc, 128) for o in colchunks]

        m1 = ar.mark()
        xn = [ar.bf16(DC, TS) for _ in range(SPT)]
        hid = [ar.bf16(FC, TS) for _ in range(SPT)]
        sq = ar.bf16(DC, TS)
        sil = [ar.f32(TS) for _ in range(2)]
        w13 = WStream(P, ar, "w13", 4, DC, 128, group=2)
        w2s = WStream(P, ar, "w2", 3, FC, 128)
        wk8 = WStream(P, ar, "wk8", 4, DC, 128, group=2)
        for st in range(NST):
            w13.plan(w13_units(0))
            w2s.plan(w2_units(0))
            wk8.plan(wcols_units(w_in, DC, [8, 9, 10, 11]))
            cu = []
            for q in range(4):
                cu += wcols_units(w_in, DC, [q, 4 + q])
            wk8.plan(cu)
        for st in range(NST):
            subs = [st * SPT + i for i in range(SPT)]
            ffn(st, 0, xn, hid, sq, sil, w13, w2s)
            wk8.prefetch()
            for i, sub in enumerate(subs):
                rmsnorm_to(sub, V_MIXN, xn[i], ("xn", i), sq)
            for q in range(4):
                wu, ku = wk8.get()
                for i, sub in enumerate(subs):
                    b = bank([0, 1, 2, 3, 4, 5])
                    mm_group(b, banks[b][:, 0:TS], [(wu[:, k, :], xn[i][:, k, :], (ku, ("xn", i))) for k in range(DC)])
                    P.op("act", lambda e, b=b, q=q, sub=sub: e.activation(ufm[:, q, cols(sub)], banks[b][:, 0:TS], AF.Copy),
                         reads=(("ps", b),), writes=(("ufm", sub),))
            for q in range(4):
                wv_, kv = wk8.get()
                wg_, kg = wk8.get()
                for i, sub in enumerate(subs):
                    bv = bank([0, 1, 2, 3, 4, 5])
                    bg = bank([0, 1, 2, 3, 4, 5])
                    mm_group(bv, banks[bv][:, 0:TS], [(wv_[:, k, :], xn[i][:, k, :], (kv, ("xn", i))) for k in range(DC)])
                    mm_group(bg, banks[bg][:, 0:TS], [(wg_[:, k, :], xn[i][:, k, :], (kg, ("xn", i))) for k in range(DC)])
                    sl = sil[(q * SPT + i) % 2]
                    skey = ("sil", (q * SPT + i) % 2)
                    P.op("act", lambda e, sl=sl, bg=bg: e.activation(sl, banks[bg][:, 0:TS], AF.Sigmoid),
                         reads=(("ps", bg),), writes=(skey,))
                    zc = slice(HALO + sub * TS, HALO + (sub + 1) * TS)
                    P.op("dve", lambda e, sl=sl, bv=bv, q=q, zc=zc: e.tensor_tensor(zall[:, q, zc], sl, banks[bv][:, 0:TS], ALU.mult),
                         reads=(skey, ("ps", bv)), writes=(("z", sub),))
        P.barrier()
        dump("h1", hres)
        dump("u", ufm)
        dump("z", zall)
        ar.release(m1)

        m2 = ar.mark()
        eraw = ar.f32(4096)
        esel = eraw.bitcast(BF16).rearrange("p (a b) -> p a b", a=64)
        Pst = ar.f32(64)
        W2t = ar.f32(64)
        esem = P.dsem("e")
        P.op("pool", lambda e: e.dma_start(out=esel, in_=esel_d.rearrange("p (a b) -> p a b", a=64)),
             writes=("esel",), dsem=esem)
        P.barrier()

        ug = ar.bf16(G, NCH)
        vxs = ar.bf16(G, NCH + 1)
        vy = ar.bf16(G, NCH)
        gz = ar.f32(6 * TS)
        gath = gz[:, 0:2 * 192].rearrange("p (a b) -> p a b", a=2)
        gtmp = [[gz[:, (3 * a + b) * TS:(3 * a + b + 1) * TS] for b in range(3)] for a in range(2)]
        pay = ar.f32(192)
        acc = ar.f32(192)
        NB, KB = 16, 16
        Rall = ar.f32(2, G, NB + 1)
        ucont = vy.rearrange("p g c -> p (g c)").rearrange("p (q j c) -> p q j c", q=4, j=8)
        for q in range(4):
            for j in range(8):
                if (q * 8 + j) % 2 == 0:
                    P.op("act", lambda e, q=q, j=j: e.activation(ucont[:, q, j, :], ufm[:, q, j::8], AF.Copy), writes=("vy",))
                else:
                    P.op("dve", lambda e, q=q, j=j: e.tensor_copy(ucont[:, q, j, :], ufm[:, q, j::8]), writes=("vy",))
        for g in range(G):
            q, gl = g // 8, g % 8
            bk = bank([0, 1, 2, 3, 4, 5])
            terms = []
            for kb in range(8):
                j = 7 - kb
                terms.append((esel[:, gl * 8 + kb, :], ucont[:, q, j, :], ("vy",)))
            mm_group(bk, banks[bk][:, 0:NCH], terms)
            eng = "act" if g % 2 == 0 else "dve"
            if eng == "act":
                P.op("act", lambda e, g=g, bk=bk: e.activation(ug[:, g, :], banks[bk][:, 0:NCH], AF.Copy),
                     reads=(("ps", bk),), writes=(("ug", g),))
            else:
                P.op("dve", lambda e, g=g, bk=bk: e.tensor_copy(ug[:, g, :], banks[bk][:, 0:NCH]),
                     reads=(("ps", bk),), writes=(("ug", g),))
        for g in range(G):
            bx = bank([4, 5, 6, 7])
            by = bank([4, 5, 6, 7])
            mm_group(bx, banks[bx][:, 0:NCH], [(wst3[:, g, 0:128], ug[:, g, :], (("ug", g),))])
            mm_group(by, banks[by][:, 0:NCH], [(wst3[:, g, 64:192], ug[:, g, :], (("ug", g),))])
            P.op("act", lambda e, g=g, bx=bx: e.activation(vxs[:, g, 1:NCH + 1], banks[bx][:, 0:NCH], AF.Copy),
                 reads=(("ps", bx),), writes=("vxs",))
            P.op("dve", lambda e, g=g, by=by: e.tensor_copy(vy[:, g, :], banks[by][:, 0:NCH]),
                 reads=(("ps", by),), writes=("vy",))
        P.barrier()
        Lst = eraw[:, 0:1024]
        W2b = eraw[:, 1024:2048]
        CAr = eraw[:, 2048:3072]
        CBr = eraw[:, 3072:4096]
        pstr = list(Pst.ap[0])

        def v4(a):
            return a.rearrange("p (a g b) -> p a g b", a=2, g=G)

        def v3(a):
            return a.rearrange("p (a n) -> p a n", a=2)

        def swp(a):
            return bass.AP(a.tensor, a.offset + 512, [list(a.ap[0]), [-512, 2], [1, 512]])

        S = lambda fn, r=("scan",), w=("scan",), eng="dve": P.op(eng, fn, reads=r, writes=w)
        S(lambda e: e.tensor_copy(CAr.rearrange("p (n b) -> p n b", b=NB), CA.unsqueeze(2).broadcast_to([128, 64, NB])))
        S(lambda e: e.tensor_copy(CBr.rearrange("p (n b) -> p n b", b=NB), CB.unsqueeze(2).broadcast_to([128, 64, NB])))
        S(lambda e: e.memset(Lst, 0.0))
        for j in range(KB):
            S(lambda e: e.tensor_tensor(v3(W2b), v3(CBr), swp(Lst), ALU.mult))
            S(lambda e: e.tensor_tensor(Lst, Lst, CAr, ALU.mult), r=("scan", "lwb"))
            S(lambda e: e.tensor_tensor(Lst, Lst, W2b, ALU.add))
            S(lambda e, j=j: e.tensor_tensor(v4(Lst)[:, 0], v4(Lst)[:, 0], vxs[:, :, 1 + j:1 + j + (NB - 1) * KB + 1:KB], ALU.add),
              r=("scan", "vxs", "vy"))
            S(lambda e, j=j: e.tensor_tensor(v4(Lst)[:, 1], v4(Lst)[:, 1], vy[:, :, j:j + (NB - 1) * KB + 1:KB], ALU.add))
            P.op("act", lambda e, j=j: e.activation(vxs[:, :, 1 + j:1 + j + (NB - 1) * KB + 1:KB], v4(Lst)[:, 0], AF.Copy),
                 reads=("scan",), writes=("lwb", "vxs"))
        Rv = Rall
        Pv = Pst.rearrange("p (a b) -> p a b", a=2)
        W2v = W2t.rearrange("p (a b) -> p a b", a=2)
        CBv = CB.rearrange("p (a b) -> p a b", a=2)
        CAv = CA.rearrange("p (a b) -> p a b", a=2)
        CB16v = CB16.rearrange("p (a b) -> p a b", a=2)
        CA16v = CA16.rearrange("p (a b) -> p a b", a=2)
        RSTR = G * (NB + 1)

        def rsw(b):
            r0 = Rall[:, 0, :, b]
            return bass.AP(r0.tensor, r0.offset + RSTR, [list(r0.ap[0]), [-RSTR, 2], [NB + 1, G]])

        def block_scan(init):
            if init is None:
                S(lambda e: e.memset(Rall[:, :, :, 0], 0.0))
            else:
                S(lambda e: e.tensor_copy(Rall[:, :, :, 0], init.rearrange("p (a b) -> p a b", a=2)), r=("scan", "acc"))
            for b in range(NB):
                S(lambda e, b=b: e.tensor_tensor(W2v, CB16v, rsw(b), ALU.mult))
                S(lambda e, b=b: e.tensor_tensor(Pv, CA16v, Rall[:, :, :, b], ALU.mult))
                S(lambda e: e.tensor_tensor(Pst, Pst, W2t, ALU.add))
                S(lambda e, b=b: e.tensor_tensor(Rall[:, :, :, b + 1], Pv, v4(Lst)[:, :, :, b], ALU.add))

        block_scan(None)
        S(lambda e: e.tensor_tensor(W2v, CBv, rsw(NB), ALU.mult))
        S(lambda e: e.tensor_tensor(Pv, CAv, Rall[:, :, :, NB], ALU.mult))
        S(lambda e: e.tensor_tensor(Pst, Pst, W2t, ALU.add))
        S(lambda e: e.tensor_tensor(Pst[:, 0:32], Pst[:, 0:32], vxs[:, :, NCH], ALU.add), r=("scan", "vxs"))
        S(lambda e: e.tensor_tensor(Pst[:, 32:64], Pst[:, 32:64], vy[:, :, NCH - 1], ALU.add), r=("scan", "vy"))
        P.op("dve", lambda e: e.tensor_copy(pay[:, 0:64], Pst), reads=("scan",), writes=("pay",))
        P.op("dve", lambda e: e.tensor_copy(pay[:, 64:192].rearrange("p (a b) -> p a b", a=4), zall[:, :, NT:NT + HALO]),
             reads=(("z", NSUB - 1),), writes=("pay",))
        xs = P.dsem("xc")
        P.op("pool", lambda e: e.dma_start(out=cc_in[:, :], in_=pay), reads=("pay",), writes=("ccin",), dsem=xs)
        P.op("pool", lambda e: e.collective_compute("AllGather", ALU.bypass, replica_groups=[[2 * b, 2 * b + 1] for b in range(NCORES // 2)],
                                                    ins=[cc_in.ap().opt()], outs=[cc_out.ap().opt()]),
             reads=("ccin",), writes=("ccout",))
        P.op("pool", lambda e: e.dma_start(out=gath, in_=cc_out.ap().rearrange("(r p) n -> p r n", p=128)),
             reads=("ccout",), writes=("gath",), dsem=xs)
        P.op("dve", lambda e: e.tensor_scalar(acc, gath[:, 0, :], onehot[:, 0:1], None, ALU.mult), reads=("gath",), writes=("acc",))
        for r in range(1, 2):
            P.op("dve", lambda e, r=r: e.scalar_tensor_tensor(acc, gath[:, r, :], onehot[:, r:r + 1], acc, ALU.mult, ALU.add),
                 reads=("gath", "acc"), writes=("acc",))
        P.op("dve", lambda e: e.tensor_copy(zall[:, :, 0:HALO], acc[:, 64:192].rearrange("p (a b) -> p a b", a=4)),
             reads=("acc",), writes=("zhalo",))
        P.barrier()
        block_scan(acc[:, 0:64])
        S(lambda e: e.tensor_copy(v4(W2b)[:, 0], Rall[:, 0, :, 0:NB]))
        S(lambda e: e.tensor_copy(v4(W2b)[:, 1], Rall[:, 1, :, 0:NB]))
        S(lambda e: e.tensor_copy(vxs[:, :, 0], Rall[:, 0, :, 0]), r=("scan", "vxs"), w=("scan", "vxs"))
        for j in range(KB):
            S(lambda e: e.tensor_tensor(v3(Lst), v3(CBr), swp(W2b), ALU.mult))
            S(lambda e: e.tensor_tensor(W2b, W2b, CAr, ALU.mult))
            S(lambda e: e.tensor_tensor(W2b, W2b, Lst, ALU.add))
            S(lambda e, j=j: e.tensor_tensor(vxs[:, :, 1 + j:1 + j + (NB - 1) * KB + 1:KB], vxs[:, :, 1 + j:1 + j + (NB - 1) * KB + 1:KB],
                                             v4(W2b)[:, 0], ALU.add), r=("scan", "vxs"), w=("scan", "vxs"))
        P.barrier()
        P.op("pool", lambda e: e.dma_start(out=esel, in_=esel_d.rearrange("p (a b) -> p a b", a=64)),
             writes=("esel",), dsem=esem)
        for g in range(G):
            bk = bank([0, 1, 2, 3, 4, 5])
            mm_group(bk, banks[bk][:, 0:NCH], [(mbf[:, g, :], ug[:, g, :], (("ug", g), "mbf")),
                                               (woutb[:, g, :], vxs[:, g, 0:NCH], ("vxs",))])
            P.op("act", lambda e, g=g, bk=bk: e.activation(ug[:, g, :], banks[bk][:, 0:NCH], AF.Copy),
                 reads=(("ps", bk),), writes=(("ug", g),))
        for q in range(4):
            for t in range(8):
                bk = bank([4, 5, 6, 7])
                mm_group(bk, banks[bk][:, 0:NCH],
                         [(esel[:, t * 8 + gl, :], ug[:, q * 8 + gl, :], (("ug", q * 8 + gl), "esel")) for gl in range(8)])
                ps = banks[bk][:, 0:NCH]
                par = (q * 8 + t) % 2
                a0, a1, a2 = gtmp[par]
                k0, k1, k2 = ("g0", par), ("g1", par), ("g2", par)
                P.op("act", lambda e, ps=ps, a0=a0: e.activation(a0, ps, AF.Square, scale=math.sqrt(0.044715)),
                     reads=(("ps", bk),), writes=(k0,))
                P.op("dve", lambda e, ps=ps, a0=a0, a1=a1: e.scalar_tensor_tensor(a1, a0, 1.0, ps, ALU.add, ALU.mult),
                     reads=(k0, ("ps", bk)), writes=(k1,))
                P.op("act", lambda e, a1=a1, a2=a2: e.activation(a2, a1, AF.Sigmoid, scale=2.0 * math.sqrt(2.0 / math.pi)),
                     reads=(k1,), writes=(k2,))
                P.op("dve", lambda e, ps=ps, q=q, t=t, a2=a2: e.tensor_tensor(ufm[:, q, t::8], a2, ps, ALU.mult),
                     reads=(k2, ("ps", bk)), writes=("yg",))
        P.barrier()
        dump("yg", ufm)
        dump("mbf", mbf)
        dump("wst3", wst3)
        dump("woutb", woutb)
        dump("ug", ug)
        dump("vxs", vxs)
        ar.release(m2)

        ar.size = ARENA
        m5 = ar.mark()
        xn = [ar.bf16(DC, TS) for _ in range(SPT)]
        sq = ar.bf16(DC, TS)
        sil = [ar.f32(TS) for _ in range(2)]
        cacc = [ar.f32(4, TS) for _ in range(SPT)]
        cs = [ar.bf16(4, TS) for _ in range(SPT)]
        cab = ar.bf16(4, TS)
        csq = ar.bf16(4, TS)
        m4 = ar.mark()
        gates = [ar.bf16(16, TS) for _ in range(SPT)]
        mA = ar.mark()
        ar.release(m4)
        hid = [ar.bf16(FC, TS) for _ in range(SPT)]
        ar.off = max(ar.off, mA)
        mrg = [ar.bf16(DC, TS) for _ in range(SPT)]
        ysb = [ar.f32(TS) for _ in range(2)]
        mt = [ar.f32(TS) for _ in range(2)]
        lnm = ar.f32(TS)
        lnv = ar.f32(TS)
        w13 = WStream(P, ar, "v13", 4, DC, 128, group=2)
        w2s = WStream(P, ar, "v2", 2, FC, 128)
        wk8 = WStream(P, ar, "vk8", 6, DC, 128)
        wk4 = WStream(P, ar, "vk4", 9, 4, 128, group=3)
        for st in range(NST):
            wk8.plan(wcols_units(w_in, DC, list(range(12, 28))))
            u4 = []
            for d in range(DC):
                u4 += wcols_units(conv_proj, 4, [d]) + wcols_units(w_v, 4, [d]) + wcols_units(w_g, 4, [d])
            wk4.plan(u4)
            wk8.plan(wcols_units(w_out, DC, list(range(DC))))
            w13.plan(w13_units(1))
            w2s.plan(w2_units(1))
        def conv_thunks(st):
            subs = [st * SPT + i for i in range(SPT)]
            T = []
            for k in range(31):
                for i, sub in enumerate(subs):
                    for q in range(4):
                        zc = slice(sub * TS + 2 + k, sub * TS + 2 + k + TS)
                        wcol = vecs[:, V_DW + q * 31 + k:V_DW + q * 31 + k + 1]
                        if k == 0:
                            T.append(lambda i=i, q=q, zc=zc, wcol=wcol, sub=sub: P.op("dve", lambda e: e.tensor_scalar(
                                cacc[i][:, q, :], zall[:, q, zc], wcol, vecs[:, V_DWB + q:V_DWB + q + 1], ALU.mult, ALU.add),
                                reads=(("z", sub), ("z", sub - 1), "zhalo", ("cs", i)), writes=(("cacc", i, q),)))
                        else:
                            T.append(lambda i=i, q=q, zc=zc, wcol=wcol: P.op("dve", lambda e: e.scalar_tensor_tensor(
                                cacc[i][:, q, :], zall[:, q, zc], wcol, cacc[i][:, q, :], ALU.mult, ALU.add),
                                reads=(("cacc", i, q),), writes=(("cacc", i, q),)))
            for i, sub in enumerate(subs):
                ck = tuple(("cacc", i, q) for q in range(4))
                T.append(lambda i=i, ck=ck: P.op("act", lambda e: e.activation(cab, cacc[i], AF.Copy), reads=ck, writes=("cab",)))
                T.append(lambda i=i, ck=ck: P.op("act", lambda e: e.activation(csq, cacc[i], AF.Square), reads=ck, writes=("csq",)))
                T.append(lambda: mm_group(6, banks[6][:, 0:TS], [(onesb, cab[:, q, :], ("cab",)) for q in range(4)]))
                T.append(lambda: P.op("dve", lambda e: e.tensor_scalar(lnm, banks[6][:, 0:TS], 1.0 / 512, None, ALU.mult),
                                      reads=(("ps", 6),), writes=("lnm",)))
                T.append(lambda: mm_group(6, banks[6][:, 0:TS], [(onesb, csq[:, q, :], ("csq",)) for q in range(4)]))
                T.append(lambda: P.op("dve", lambda e: e.tensor_tensor(lnv, lnm, lnm, ALU.mult), reads=("lnm",), writes=("lnv",)))
                T.append(lambda: P.op("dve", lambda e: e.scalar_tensor_tensor(lnv, banks[6][:, 0:TS], 1.0 / 512, lnv, ALU.mult, ALU.subtract),
                                      reads=(("ps", 6), "lnv"), writes=("lnv",)))
                T.append(lambda: P.op("dve", lambda e: e.tensor_scalar(lnv, lnv, EPS, None, ALU.add), reads=("lnv",), writes=("lnv",)))
                T.append(lambda: P.op("act", lambda e: e.activation(lnv, lnv, AF.Sqrt), reads=("lnv",), writes=("lnv",)))
                T.append(lambda: P.op("dve", lambda e: e.reciprocal(lnv, lnv), reads=("lnv",), writes=("lnv",)))
                for q in range(4):
                    T.append(lambda i=i, q=q: P.op("dve", lambda e: e.tensor_tensor(cacc[i][:, q, :], cacc[i][:, q, :], lnm, ALU.subtract),
                                                   reads=(("cacc", i, q), "lnm"), writes=(("cacc", i, q),)))
                    T.append(lambda i=i, q=q: P.op("dve", lambda e: e.tensor_tensor(cacc[i][:, q, :], cacc[i][:, q, :], lnv, ALU.mult),
                                                   reads=(("cacc", i, q), "lnv"), writes=(("cacc", i, q),)))
                    T.append(lambda i=i, q=q: P.op("act", lambda e: e.activation(cs[i][:, q, :], cacc[i][:, q, :], AF.Silu,
                                                                                 bias=vecs[:, V_LNB + q:V_LNB + q + 1],
                                                                                 scale=vecs[:, V_LNG + q:V_LNG + q + 1]),
                                                   reads=(("cacc", i, q),), writes=(("cs", i),)))
            return T

        cur = conv_thunks(0)
        nxt = []

        def fill(n):
            for _ in range(n):
                if cur:
                    cur.pop(0)()
                elif nxt:
                    nxt.pop(0)()
                else:
                    return

        def drain_cur():
            while cur:
                cur.pop(0)()

        for st in range(NST):
            subs = [st * SPT + i for i in range(SPT)]
            if st + 1 < NST:
                nxt.extend(conv_thunks(st + 1))
            wk8.prefetch()
            wk4.prefetch()
            for i, sub in enumerate(subs):
                rmsnorm_to(sub, V_MIXN, xn[i], ("xn", i), sq)
            for o in range(16):
                wu, ku = wk8.get()
                for i, sub in enumerate(subs):
                    b = bank([0, 1, 2, 3, 4, 5])
                    mm_group(b, banks[b][:, 0:TS], [(wu[:, k, :], xn[i][:, k, :], (ku, ("xn", i))) for k in range(DC)])
                    P.op("act", lambda e, b=b, o=o, i=i: e.activation(gates[i][:, o, :], banks[b][:, 0:TS], AF.Sigmoid,
                                                                     bias=vecs[:, V_BG + o:V_BG + o + 1]),
                         reads=(("ps", b),), writes=(("gates", i), "gh") if o == 0 else (("gates", i),))
                    fill(1)
            drain_cur()
            for d in range(DC):
                wc_, kc_ = wk4.get()
                wv_, kv = wk4.get()
                wg_, kg = wk4.get()
                for i, sub in enumerate(subs):
                    c = cols(sub)
                    bc_ = bank([0, 1, 2, 3, 4, 5])
                    bv = bank([0, 1, 2, 3, 4, 5])
                    bg = bank([0, 1, 2, 3, 4, 5])
                    mm_group(bc_, banks[bc_][:, 0:TS], [(wc_[:, k, :], cs[i][:, k, :], (kc_, ("cs", i))) for k in range(4)])
                    mm_group(bv, banks[bv][:, 0:TS], [(wv_[:, k, :], ufm[:, k, c], (kv, "yg")) for k in range(4)])
                    mm_group(bg, banks[bg][:, 0:TS], [(wg_[:, k, :], ufm[:, k, c], (kg, "yg")) for k in range(4)])
                    j = (d * SPT + i) % 2
                    P.op("act", lambda e, j=j, bg=bg: e.activation(ysb[j], banks[bg][:, 0:TS], AF.Sigmoid),
                         reads=(("ps", bg),), writes=(("ysb", j),))
                    P.op("dve", lambda e, j=j, bv=bv: e.tensor_tensor(ysb[j], ysb[j], banks[bv][:, 0:TS], ALU.mult),
                         reads=(("ysb", j), ("ps", bv)), writes=(("ysb", j),))
                    P.op("dve", lambda e, j=j, i=i, d=d, bc_=bc_: e.tensor_tensor(mt[j], gates[i][:, d, :], banks[bc_][:, 0:TS], ALU.mult),
                         reads=(("gates", i), "gh", ("ps", bc_)), writes=(("mt", j),))
                    P.op("dve", lambda e, j=j, i=i, d=d: e.tensor_tensor(ysb[j], ysb[j], gates[i][:, 8 + d, :], ALU.mult),
                         reads=(("ysb", j), ("gates", i), "gh"), writes=(("ysb", j),))
                    P.op("dve", lambda e, j=j, i=i, d=d: e.tensor_tensor(mrg[i][:, d, :], mt[j], ysb[j], ALU.add),
                         reads=(("mt", j), ("ysb", j)), writes=(("mrg", i),))
            w13.prefetch()
            for d in range(DC):
                wo, ko = wk8.get()
                for i, sub in enumerate(subs):
                    c = cols(sub)
                    bo = bank([0, 1, 2, 3])
                    mm_group(bo, banks[bo][:, 0:TS], [(wo[:, k, :], mrg[i][:, k, :], (ko, ("mrg", i))) for k in range(DC)])
                    P.op("dve", lambda e, bo=bo, d=d, c=c: e.tensor_tensor(hres[:, d, c], hres[:, d, c], banks[bo][:, 0:TS], ALU.add),
                         reads=(("ps", bo), hkey(sub)), writes=(hkey(sub),))
                    fill(1)
            ffn(st, 1, xn, hid, sq, sil, w13, w2s, fill=fill)
            cur.extend(nxt)
            del nxt[:]
        P.barrier()
        ar.release(m5)

        fsq = [ar.bf16(DC, 128) for _ in range(2)]
        frs = [ar.f32(128) for _ in range(2)]
        fon = [ar.f32(DC, 128) for _ in range(2)]
        oblk = [ar.f32(D) for _ in range(2)]
        osem = [P.dsem("o") for _ in range(2)]

        def fin_a(bi):
            r0 = bi * 128
            nr = min(128, NT - r0)
            s = bi % 2
            c = slice(r0, r0 + nr)
            sb = 4 + s
            P.op("act", lambda e: e.activation(fsq[s][:, :, 0:nr], hres[:, :, c], AF.Square), writes=(("fsq", s),))
            mm_group(sb, banks[sb][:, 0:nr], [(onesb, fsq[s][:, k, 0:nr], (("fsq", s),)) for k in range(DC)])
            P.op("dve", lambda e: e.tensor_scalar(frs[s][:, 0:nr], banks[sb][:, 0:nr], 1.0 / D, EPS, ALU.mult, ALU.add),
                 reads=(("ps", sb),), writes=(("frs", s),))
            P.op("act", lambda e: e.activation(frs[s][:, 0:nr], frs[s][:, 0:nr], AF.Sqrt), reads=(("frs", s),), writes=(("frs", s),))
            P.op("dve", lambda e: e.reciprocal(frs[s][:, 0:nr], frs[s][:, 0:nr]), reads=(("frs", s),), writes=(("frs", s),))
            for k in range(DC):
                P.op("dve", lambda e, k=k: e.scalar_tensor_tensor(
                    fon[s][:, k, 0:nr], hres[:, k, c], vecs[:, V_FIN + k:V_FIN + k + 1], frs[s][:, 0:nr], ALU.mult, ALU.mult),
                    reads=(("frs", s),), writes=(("fon", s),))

        def fin_b(bi):
            r0 = bi * 128
            nr = min(128, NT - r0)
            s = bi % 2
            for hb in range(2):
                bk = 2 * s + hb
                for kk in range(4):
                    k = hb * 4 + kk
                    P.op("pe", lambda e, bk=bk, kk=kk, k=k: e.transpose(banks[bk][0:nr, kk * 128:(kk + 1) * 128], fon[s][:, k, 0:nr], ident),
                         reads=(("fon", s),), writes=(("ps", bk),) if kk in (0, 3) else (), signal=(kk == 3))
                P.op("act", lambda e, bk=bk, hb=hb: e.activation(oblk[s][0:nr, hb * 512:(hb + 1) * 512], banks[bk][0:nr, :], AF.Copy),
                     reads=(("ps", bk),), writes=(("oblk", s),))
            P.op("sp", lambda e: e.dma_start(out=out_d[r0:r0 + nr, :], in_=oblk[s][0:nr, :]),
                 reads=(("oblk", s),), dsem=osem[s])

        fin_a(0)
        for bi in range(nblk):
            if bi + 1 < nblk:
                fin_a(bi + 1)
            fin_b(bi)
        P.barrier()
        P.emit()
    return nc


NVEC = 60 + 124
NSP = 136 + 2048

_CACHE = {}
_DBG = False
_DBG_HOOK = None


def _consts():
    ident = np.eye(128, dtype=np.float32)
    r = np.arange(128)
    kb, t = r[:, None] // 16, r[None, :] // 16
    mask = ((kb + t) >= 7).astype(np.float32)
    adi = (((kb + t) == 7) & ((r[:, None] % 16) == (r[None, :] % 16))).astype(np.float32)
    cst = np.concatenate([ident, mask, adi, np.zeros((128, 128), np.float32)], axis=1)
    esel = np.zeros((128, 64, 128), np.float32)
    for a in range(8):
        for b in range(8):
            for h in range(16):
                esel[a * 16 + h, a * 8 + b, b * 16 + h] = 1.0
    return np.ascontiguousarray(cst), np.ascontiguousarray(esel.reshape(128, 64 * 128))


def kernel(x, meta_tokens, ffn1_norm, ffn1_w1, ffn1_w3, ffn1_w2, mix_norm, w_in, b_gate,
           conv_dw, conv_dw_b, conv_ln_g, conv_ln_b, conv_proj,
           ssm_lam_re, ssm_lam_im, ssm_log_dt, ssm_b_re, ssm_b_im, ssm_c_re, ssm_c_im,
           ssm_d, ssm_w_v, ssm_w_g, w_out, ffn2_norm, ffn2_w1, ffn2_w3, ffn2_w2, final_norm):
    f = lambda a: np.ascontiguousarray(np.asarray(a, dtype=np.float32))
    x = f(x)
    B = x.shape[0]
    def colv(v, nchunk):
        return f(v).reshape(nchunk, 128).T
    vecs = np.concatenate([
        colv(ffn1_norm[0], 8), colv(mix_norm[0], 8), colv(ffn2_norm[0], 8), colv(final_norm, 8),
        colv(b_gate[0], 16), colv(conv_dw_b[0], 4), colv(conv_ln_g[0], 4), colv(conv_ln_b[0], 4),
        f(conv_dw[0]).T.reshape(4, 128, 31).transpose(1, 0, 2).reshape(128, 124),
    ], axis=1)
    assert vecs.shape == (128, NVEC)
    dup = lambda a: np.concatenate([a, a], axis=0)
    lre = dup(f(ssm_lam_re[0]).T)
    lim = dup(f(ssm_lam_im[0]).T)
    ldt = np.broadcast_to(f(ssm_log_dt[0])[None, :], (128, G))
    dcol = np.tile(f(ssm_d[0]).reshape(G, 16).T, (8, 1))
    bre = dup(f(ssm_b_re[0]).transpose(1, 0, 2).reshape(64, G * 16))
    bim = dup(f(ssm_b_im[0]).transpose(1, 0, 2).reshape(64, G * 16))
    cre = dup(f(ssm_c_re[0]).transpose(2, 0, 1).reshape(64, G * 16))
    cim = dup(f(ssm_c_im[0]).transpose(2, 0, 1).reshape(64, G * 16))
    cst, esel = _consts()

    def tile_w(w):
        w = f(w)
        K, N = w.shape
        return np.ascontiguousarray(w.reshape(K // 128, 128, N // 128, 128).transpose(2, 1, 0, 3).reshape(N // 128, 128, K))
    wt = {"ffn1_w1": tile_w(ffn1_w1[0]), "ffn1_w3": tile_w(ffn1_w3[0]), "ffn1_w2": tile_w(ffn1_w2[0]),
          "ffn2_w1": tile_w(ffn2_w1[0]), "ffn2_w3": tile_w(ffn2_w3[0]), "ffn2_w2": tile_w(ffn2_w2[0]),
          "w_in": tile_w(w_in[0]), "conv_proj": tile_w(conv_proj[0]), "ssm_w_v": tile_w(ssm_w_v[0]),
          "ssm_w_g": tile_w(ssm_w_g[0]), "w_out": tile_w(w_out[0])}
    meta = f(meta_tokens)
    half = NT - 16
    in_maps = []
    for core in range(NCORES):
        b, s = core // 2, core % 2
        if s == 0:
            xin = np.concatenate([meta, x[b, 0:half]], axis=0)
        else:
            xin = x[b, half:]
        oh = np.zeros((128, 8), np.float32)
        if s == 1:
            oh[:, 0] = 1.0
        sp = np.concatenate([lre, lim, ldt, dcol, oh, bre, bim, cre, cim], axis=1)
        assert sp.shape == (128, NSP)
        in_maps.append({
            "xin": f(xin), "vecs": f(vecs), "sp": f(sp), "cst": cst, "esel": esel, **wt,
        })
    if "nc" not in _CACHE:
        nc = bass.Bass("TRN2", target_bir_lowering=False)
        _CACHE["nc"] = build(nc)
    res = run_bass_kernel_spmd(_CACHE["nc"], in_maps, core_ids=list(range(NCORES)))
    if _DBG_HOOK is not None:
        _DBG_HOOK(res)
    out = np.empty((B, 4096, D), np.float32)
    for core in range(NCORES):
        b, s = core // 2, core % 2
        o = np.asarray(res.results[core]["out"], dtype=np.float32)
        if s == 0:
            out[b, 0:half] = o[16:]
        else:
            out[b, half:] = o
    return out
```

```python
import math
from contextlib import ExitStack

import numpy as np
import concourse.bass as bass
import concourse.mybir as mybir
from concourse.bass_utils import run_bass_kernel_spmd

F32 = mybir.dt.float32
BF16 = mybir.dt.bfloat16
AF = mybir.ActivationFunctionType
ALU = mybir.AluOpType

NCORES = 8
NT = 2056
TS = 257
NSUB = 8
SPT = 2
NST = NSUB // SPT
D = 1024
DC = 8
DFF = 2816
FC = 22
NCH = 257
G = 32
HALO = 32
EPS = 1e-6
ARENA = 53200


class Tok:
    __slots__ = ("sem", "val", "eng")

    def __init__(self, sem, val, eng):
        self.sem, self.val, self.eng = sem, val, eng


class Prog:
    def __init__(self, nc, es):
        self.nc, self.es = nc, es
        self.engs = {"pe": nc.tensor, "act": nc.scalar, "dve": nc.vector, "pool": nc.gpsimd, "sp": nc.sync}
        self.ops = {e: [] for e in self.engs}
        self.cur = {}
        self.waited = {e: {} for e in self.engs}
        self.lastw = {}
        self.readers = {}
        self.pend_r, self.pend_w = [], []
        self.nsem = 0
        self.dsems = []

    def newsem(self, name):
        self.nsem += 1
        return self.es.enter_context(self.nc.semaphore(f"{name}{self.nsem}"))

    def dsem(self, name="d"):
        d = [self.newsem(name), 0]
        self.dsems.append(d)
        return d

    def engsem(self, e):
        c = self.cur.get(e)
        if c is None or c[1] >= 30000:
            c = [self.newsem("s" + e), 0]
            self.cur[e] = c
        return c

    def _need(self, e, toks):
        need = {}
        for t in toks:
            if e == "pe" and t.eng == "pe":
                continue
            sid = id(t.sem)
            if t.val <= self.waited[e].get(sid, 0):
                continue
            if sid not in need or need[sid].val < t.val:
                need[sid] = t
        for sid, t in need.items():
            self.waited[e][sid] = t.val
        return list(need.values())

    def op(self, e, fn, reads=(), writes=(), signal=True, dsem=None, inc=1):
        toks = []
        for k in reads:
            t = self.lastw.get(k)
            if t is not None:
                toks.append(t)
        for k in writes:
            t = self.lastw.get(k)
            if t is not None:
                toks.append(t)
            toks.extend(self.readers.get(k, {}).values())
        waits = self._need(e, toks)
        tok, sig = None, None
        if dsem is not None:
            dsem[1] += 16
            tok = Tok(dsem[0], dsem[1], "dma")
            sig = (dsem[0], 16)
        elif signal:
            c = self.engsem(e)
            c[1] += inc
            tok = Tok(c[0], c[1], e)
            sig = (c[0], inc)
        self.ops[e].append((waits, fn, sig))
        if e == "pe" and tok is None:
            self.pend_r.extend(reads)
            self.pend_w.extend(writes)
            return None
        if e == "pe":
            reads = list(reads) + self.pend_r
            writes = list(writes) + self.pend_w
            self.pend_r, self.pend_w = [], []
        for k in writes:
            self.lastw[k] = tok
            self.readers[k] = {}
        for k in reads:
            r = self.readers.setdefault(k, {})
            sid = id(tok.sem)
            if sid not in r or r[sid].val < tok.val:
                r[sid] = tok
        return tok

    def barrier(self):
        toks = [Tok(c[0], c[1], e) for e, c in self.cur.items() if c[1] > 0]
        toks += [Tok(d[0], d[1], "dma") for d in self.dsems if d[1] > 0]
        for e in self.engs:
            w = []
            for t in toks:
                sid = id(t.sem)
                if t.val > self.waited[e].get(sid, 0):
                    self.waited[e][sid] = t.val
                    w.append(t)
            if w:
                self.ops[e].append((w, None, None))

    def emit(self):
        with self.nc.Block() as block:
            def mk(e):
                def f(eng):
                    for waits, fn, sig in self.ops[e]:
                        for t in waits:
                            eng.wait_ge(t.sem, t.val)
                        if fn is None:
                            continue
                        ins = fn(eng)
                        if sig is not None:
                            ins.then_inc(sig[0], sig[1])
                return f
            block.tensor(mk("pe"))
            block.scalar(mk("act"))
            block.vector(mk("dve"))
            block.gpsimd(mk("pool"))
            block.sync(mk("sp"))


class Arena:
    def __init__(self, t, size, base=0):
        self.t, self.size, self.off = t, size, base

    def mark(self):
        return self.off

    def release(self, m):
        self.off = m

    def f32(self, *shape):
        n = int(np.prod(shape))
        assert self.off + n <= self.size, f"arena overflow {self.off}+{n}"
        ap = self.t[:, self.off:self.off + n]
        self.off += n
        return _shape(ap, shape)

    def bf16(self, *shape):
        n = int(np.prod(shape))
        nf = (n + 1) // 2
        assert self.off + nf <= self.size, f"arena overflow {self.off}+{nf}"
        ap = self.t[:, self.off:self.off + nf].bitcast(BF16)[:, 0:n]
        self.off += nf
        return _shape(ap, shape)


def _shape(ap, shape):
    if len(shape) == 1:
        return ap
    if len(shape) == 2:
        return ap.rearrange("p (a b) -> p a b", a=shape[0])
    if len(shape) == 3:
        return ap.rearrange("p (a b c) -> p a b c", a=shape[0], b=shape[1])
    raise ValueError(shape)


class WStream:
    def __init__(self, P, ar, name, nslots, kc, ncol, group=1):
        self.P, self.name, self.n, self.group = P, name, nslots, group
        self.slots = [ar.bf16(kc, ncol) for _ in range(nslots)]
        self.sems = [P.dsem("w" + name) for _ in range(nslots)]
        self.units, self.issued, self.consumed = [], 0, 0

    def plan(self, units):
        self.units.extend(units)

    def _issue(self):
        i = self.issued
        s = i % self.n
        src, kc, ncol = self.units[i]
        dst = self.slots[s][:, 0:kc, 0:ncol].rearrange("p a b -> p (a b)")
        if kc * ncol > 2048:
            dst = dst.rearrange("p (a b) -> p a b", a=2)
            src = src.rearrange("p (a b) -> p a b", a=2)
        self.P.op("pool", lambda e, d=dst, r=src: e.dma_start(out=d, in_=r),
                  writes=((self.name, s),), dsem=self.sems[s])
        self.issued += 1

    def prefetch(self):
        while self.issued < len(self.units) and self.issued < self.consumed + self.n - (self.group - 1):
            self._issue()

    def get(self):
        self.prefetch()
        i = self.consumed
        s = i % self.n
        _, kc, ncol = self.units[i]
        self.consumed += 1
        return self.slots[s][:, 0:kc, 0:ncol], (self.name, s)


def build(nc):
    dram = {}

    def din(name, shape):
        dram[name] = nc.dram_tensor(name, list(shape), F32, kind="ExternalInput").ap()
        return dram[name]

    xin = din("xin", (NT, D))
    w1a = [din(f"ffn{i}_w1", (FC, 128, DC * 128)) for i in (1, 2)]
    w3a = [din(f"ffn{i}_w3", (FC, 128, DC * 128)) for i in (1, 2)]
    w2a = [din(f"ffn{i}_w2", (DC, 128, FC * 128)) for i in (1, 2)]
    w_in = din("w_in", (28, 128, DC * 128))
    conv_proj = din("conv_proj", (DC, 128, 4 * 128))
    w_v = din("ssm_w_v", (DC, 128, 4 * 128))
    w_g = din("ssm_w_g", (DC, 128, 4 * 128))
    w_out = din("w_out", (DC, 128, DC * 128))
    vecs_d = din("vecs", (128, NVEC))
    sp_d = din("sp", (128, NSP))
    cst_d = din("cst", (128, 4 * 128))
    esel_d = din("esel", (128, 64 * 128))
    out_d = nc.dram_tensor("out", [NT, D], F32, kind="ExternalOutput").ap()
    cc_in = nc.dram_tensor("cc_in", [128, 192], F32)
    cc_out = nc.dram_tensor("cc_out", [2 * 128, 192], F32)

    es = ExitStack()
    with es:
        es.enter_context(nc.allow_low_precision("bf16 matmul operands, fp32 accumulation"))
        es.enter_context(nc.allow_non_contiguous_dma("layout"))
        arena_t = es.enter_context(nc.sbuf_tensor("arena", [128, ARENA], F32))
        banks = [es.enter_context(nc.psum_tensor(f"ps{i}", [128, 512], F32)) for i in range(8)]
        P = Prog(nc, es)
        ar = Arena(arena_t, ARENA)

        def wr(x):
            return dram[x] if isinstance(x, str) else x

        hres = ar.f32(DC, NT)
        ufm = ar.bf16(4, NT)
        zall = ar.bf16(4, HALO + NT)
        vecs = ar.f32(NVEC)
        ident = ar.f32(128)
        onesb = ar.bf16(128)
        rstd = ar.f32(TS)

        csem = P.dsem("c")
        DBG = _DBG
        dbg = {}
        dbgsem = P.dsem("g")

        def dump(name, ap3):
            if not DBG:
                return
            shp = list(ap3.shape)
            dbg[name] = nc.dram_tensor("dbg_" + name, shp, F32, kind="ExternalOutput").ap()
            P.barrier()
            P.op("pool", lambda e: e.dma_start(out=dbg[name], in_=ap3), dsem=dbgsem)
            P.barrier()
        P.op("sp", lambda e: e.dma_start(out=vecs, in_=vecs_d), writes=("vecs",), dsem=csem)
        P.op("sp", lambda e: e.dma_start(out=ident, in_=cst_d[:, 0:128]), writes=("ident",), dsem=csem)
        P.op("dve", lambda e: e.memset(onesb, 1.0), writes=("ones",))
        P.op("dve", lambda e: e.memset(zall[:, :, 0:HALO], 0.0), writes=("zhalo",))
        P.barrier()

        V_F1N, V_MIXN, V_F2N, V_FIN, V_BG, V_DWB, V_LNG, V_LNB, V_DW = 0, 8, 16, 24, 32, 48, 52, 56, 60

        psrr = [0]

        def bank(pool):
            b = pool[psrr[0] % len(pool)]
            psrr[0] += 1
            return b

        def mm_group(bk, out_ap, terms, extra_w=()):
            n = len(terms)
            for i, (lhsT, rhs, rk) in enumerate(terms):
                last = i == n - 1
                P.op("pe", lambda e, o=out_ap, l=lhsT, r=rhs, st=(i == 0), sp_=last:
                     e.matmul(o, l, r, start=st, stop=sp_),
                     reads=rk, writes=(("ps", bk),) if (i == 0 or last) else (), signal=last)

        TOPN = 7500
        top = Arena(arena_t, ARENA, base=ARENA - TOPN)
        ar.size = ARENA - TOPN
        wst3 = top.bf16(G, 192)
        mbf = top.bf16(G, 128)
        woutb = top.bf16(G, 128)
        CA16 = top.f32(64)
        CB16 = top.f32(64)
        CA = top.f32(64)
        CB = top.f32(64)
        onehot = top.f32(8)
        dcol = top.f32(G)
        m0 = ar.mark()
        xblk = [ar.f32(D) for _ in range(2)]
        xsem = [P.dsem("x") for _ in range(2)]
        nblk = (NT + 127) // 128
        def in_block(bi):
            r0 = bi * 128
            nr = min(128, NT - r0)
            s = bi % 2
            P.op("sp", lambda e, s=s, r0=r0, nr=nr: e.dma_start(out=xblk[s][0:nr, :], in_=xin[r0:r0 + nr, :]),
                 writes=(("xblk", s),), dsem=xsem[s])
            for hb in range(2):
                bk = 6 + hb
                for kk in range(4):
                    k = hb * 4 + kk
                    P.op("pe", lambda e, bk=bk, kk=kk, k=k, s=s, nr=nr:
                         e.transpose(banks[bk][:, kk * 128:kk * 128 + nr], xblk[s][0:nr, k * 128:(k + 1) * 128],
                                     ident[0:nr, 0:nr]),
                         reads=(("xblk", s),), writes=(("ps", bk),) if kk in (0, 3) else (), signal=(kk == 3))
                eng = "act" if hb == 0 else "dve"
                src = banks[bk][:, :].rearrange("p (a b) -> p a b", a=4)[:, :, 0:nr]
                dst = hres[:, hb * 4:hb * 4 + 4, r0:r0 + nr]
                if eng == "act":
                    P.op("act", lambda e, d=dst, s_=src: e.activation(d, s_, AF.Copy),
                         reads=(("ps", bk),), writes=(("hres", bi),))
                else:
                    P.op("dve", lambda e, d=dst, s_=src: e.tensor_copy(d, s_),
                         reads=(("ps", bk),), writes=(("hres", bi),))
        inq = list(range(nblk))

        def in_fill():
            if inq:
                in_block(inq.pop(0))

        in_fill()
        in_fill()
        mask = ar.f32(128)
        adi = ar.f32(128)
        spt = ar.f32(NSP)
        ssem = P.dsem("s")
        P.op("sp", lambda e: e.dma_start(out=mask, in_=cst_d[:, 128:256]), writes=("prep",), dsem=ssem)
        P.op("sp", lambda e: e.dma_start(out=adi, in_=cst_d[:, 256:384]), writes=("prep",), dsem=ssem)
        P.op("sp", lambda e: e.dma_start(out=spt, in_=sp_d), writes=("prep",), dsem=ssem)
        P.op("dve", lambda e: e.tensor_copy(dcol, spt[:, 96:128]), reads=("prep",), writes=("prep",))
        P.op("dve", lambda e: e.tensor_copy(onehot, spt[:, 128:136]), reads=("prep",), writes=("prep",))
        lre, lim, ldt = spt[:, 0:32], spt[:, 32:64], spt[:, 64:96]
        o0 = 136
        Bre = spt[:, o0:o0 + 512].rearrange("p (g h) -> p g h", g=G)
        Bim = spt[:, o0 + 512:o0 + 1024].rearrange("p (g h) -> p g h", g=G)
        Cre = spt[:, o0 + 1024:o0 + 1536].rearrange("p (g h) -> p g h", g=G)
        Cim = spt[:, o0 + 1536:o0 + 2048].rearrange("p (g h) -> p g h", g=G)

        pcnt = [0]

        def tick():
            pcnt[0] += 1
            if pcnt[0] % 14 == 0:
                in_fill()

        def V(fn):
            P.op("dve", fn, reads=("prep",), writes=("prep",))
            tick()

        def A(fn):
            P.op("act", fn, reads=("prep",), writes=("prep",))
            tick()

        def tt(o, a, b, op):
            V(lambda e: e.tensor_tensor(o, a, b, op))

        def ts(o, a, s1, s2, op0, op1=None):
            if op1 is None:
                V(lambda e: e.tensor_scalar(o, a, s1, None, op0))
            else:
                V(lambda e: e.tensor_scalar(o, a, s1, s2, op0, op1))

        def tmpg(n=1):
            return [ar.f32(G) for _ in range(n)]

        dt_, ar_, ai_, mag, imag, s1, c1, wtmp = tmpg(8)
        A(lambda e: e.activation(dt_, ldt, AF.Exp))
        tt(ar_, lre, dt_, ALU.mult)
        tt(ai_, lim, dt_, ALU.mult)
        A(lambda e: e.activation(mag, ar_, AF.Exp))
        A(lambda e: e.activation(imag, ar_, AF.Exp, scale=-1.0))
        TWO_PI = 2.0 * math.pi
        MAGIC = 12582912.0
        wn, = tmpg(1)

        def sin_of(o, x, shift):
            ts(wtmp, x, shift, None, ALU.add)
            ts(wn, wtmp, 1.0 / TWO_PI, MAGIC, ALU.mult, ALU.add)
            ts(wn, wn, -MAGIC, None, ALU.add)
            V(lambda e: e.scalar_tensor_tensor(wtmp, wn, -TWO_PI, wtmp, ALU.mult, ALU.add))
            ts(wn, wtmp, math.pi, TWO_PI, ALU.is_gt, ALU.mult)
            tt(wtmp, wtmp, wn, ALU.subtract)
            ts(wn, wtmp, -math.pi, TWO_PI, ALU.is_lt, ALU.mult)
            tt(wtmp, wtmp, wn, ALU.add)
            A(lambda e: e.activation(o, wtmp, AF.Sin))

        sin_of(s1, ai_, 0.0)
        sin_of(c1, ai_, 0.5 * math.pi)
        L1r, L1i, N1r, N1i = tmpg(4)
        tt(L1r, mag, c1, ALU.mult)
        tt(L1i, mag, s1, ALU.mult)
        tt(N1r, imag, c1, ALU.mult)
        V(lambda e: e.scalar_tensor_tensor(N1i, imag, -1.0, s1, ALU.mult, ALU.mult))
        PWr = ar.f32(16, G)
        PWi = ar.f32(16, G)
        V(lambda e: e.memset(PWr[:, 7, :], 1.0))
        V(lambda e: e.memset(PWi[:, 7, :], 0.0))
        t1, t2 = tmpg(2)

        def cmul(orr, oi, ar0, ai0, br0, bi0):
            tt(t1, ar0, br0, ALU.mult)
            tt(t2, ai0, bi0, ALU.mult)
            tt(orr, t1, t2, ALU.subtract)
            tt(t1, ar0, bi0, ALU.mult)
            tt(t2, ai0, br0, ALU.mult)
            tt(oi, t1, t2, ALU.add)

        for k in range(1, 9):
            cmul(PWr[:, 7 + k, :], PWi[:, 7 + k, :], PWr[:, 6 + k, :], PWi[:, 6 + k, :], L1r, L1i)
        for k in range(1, 8):
            cmul(PWr[:, 7 - k, :], PWi[:, 7 - k, :], PWr[:, 8 - k, :], PWi[:, 8 - k, :], N1r, N1i)
        nr_, den, cr, ci = tmpg(4)
        ts(nr_, L1r, -1.0, None, ALU.add)
        tt(t1, lre, lre, ALU.mult)
        tt(t2, lim, lim, ALU.mult)
        tt(den, t1, t2, ALU.add)
        V(lambda e: e.reciprocal(den, den))
        tt(t1, nr_, lre, ALU.mult)
        tt(t2, L1i, lim, ALU.mult)
        tt(cr, t1, t2, ALU.add)
        tt(cr, cr, den, ALU.mult)
        tt(t1, L1i, lre, ALU.mult)
        tt(t2, nr_, lim, ALU.mult)
        tt(ci, t1, t2, ALU.subtract)
        tt(ci, ci, den, ALU.mult)
        Bbr = ar.f32(G, 16)
        Bbi = ar.f32(G, 16)
        T1 = ar.f32(G, 16)
        T2 = ar.f32(G, 16)

        def bc(x):
            return x.unsqueeze(2).broadcast_to([128, G, 16])

        tt(T1, Bre, bc(cr), ALU.mult)
        tt(T2, Bim, bc(ci), ALU.mult)
        tt(Bbr, T1, T2, ALU.subtract)
        tt(T1, Bim, bc(cr), ALU.mult)
        tt(T2, Bre, bc(ci), ALU.mult)
        tt(Bbi, T1, T2, ALU.add)
        UA = ar.f32(8, G)
        UB = ar.f32(8, G)
        lo, hi = slice(0, 64), slice(64, 128)

        def neg(o, a):
            ts(o, a, -1.0, None, ALU.mult)

        def cp(o, a):
            V(lambda e: e.tensor_copy(o, a))

        cp(UA[lo], PWr[lo, 7:15, :])
        cp(UA[hi], PWi[hi, 7:15, :])
        neg(UB[lo], PWi[lo, 7:15, :])
        cp(UB[hi], PWr[hi, 7:15, :])
        BL = ar.f32(G, 8, 16)
        for kb in range(8):
            tt(T1, Bbr, bc(UA[:, kb, :]), ALU.mult)
            tt(T2, Bbi, bc(UB[:, kb, :]), ALU.mult)
            tt(BL[:, :, kb, :], T1, T2, ALU.add)
        WA = ar.f32(8, G)
        WB = ar.f32(8, G)
        CL = ar.f32(G, 8, 16)

        def build_cl(i0):
            cp(WA[lo], PWr[lo, i0:i0 + 8, :])
            neg(WA[hi], PWi[hi, i0:i0 + 8, :])
            neg(WB[lo], PWi[lo, i0:i0 + 8, :])
            neg(WB[hi], PWr[hi, i0:i0 + 8, :])
            for t in range(8):
                tt(T1, Cre, bc(WA[:, t, :]), ALU.mult)
                tt(T2, Cim, bc(WB[:, t, :]), ALU.mult)
                tt(CL[:, :, t, :], T1, T2, ALU.add)

        build_cl(8)
        cp(woutb, CL.rearrange("p g t h -> p g (t h)"))
        build_cl(0)
        a8, b8 = PWr[:, 15, :], PWi[:, 15, :]
        cp(CA[:, 0:32], a8)
        cp(CA[:, 32:64], a8)
        neg(CB[lo, 0:32], b8[lo])
        cp(CB[hi, 0:32], b8[hi])
        cp(CB[lo, 32:64], b8[lo])
        neg(CB[hi, 32:64], b8[hi])
        q_r, q_i, q_r2, q_i2 = tmpg(4)
        cp(q_r, a8)
        cp(q_i, b8)
        for _ in range(4):
            cmul(q_r2, q_i2, q_r, q_i, q_r, q_i)
            cp(q_r, q_r2)
            cp(q_i, q_i2)
        cp(CA16[:, 0:32], q_r)
        cp(CA16[:, 32:64], q_r)
        neg(CB16[lo, 0:32], q_i[lo])
        cp(CB16[hi, 0:32], q_i[hi])
        cp(CB16[lo, 32:64], q_i[lo])
        neg(CB16[hi, 32:64], q_i[hi])
        BLf = BL.rearrange("p g k h -> p g (k h)")
        CLf = CL.rearrange("p g t h -> p g (t h)")
        mtmp = ar.f32(128)
        for g in range(G):
            bk = 4 + g % 2
            P.op("pe", lambda e, g=g, bk=bk: e.matmul(banks[bk][:, 0:128], BLf[:, g, :], CLf[:, g, :], start=True, stop=True),
                 reads=("prep",), writes=(("ps", bk),))
            P.op("dve", lambda e, bk=bk: e.tensor_tensor(mtmp, banks[bk][:, 0:128], mask, ALU.mult),
                 reads=(("ps", bk),), writes=("mtmp",))
            P.op("dve", lambda e, g=g: e.scalar_tensor_tensor(mbf[:, g, :], adi, dcol[:, g:g + 1], mtmp, ALU.mult, ALU.add),
                 reads=("mtmp",), writes=("mbf",))
            bt = 6 + g % 2
            P.op("pe", lambda e, g=g, bt=bt: e.transpose(banks[bt][:, 0:128], BLf[:, g, :], ident),
                 reads=("prep",), writes=(("ps", bt),))
            P.op("act", lambda e, g=g, bt=bt: e.activation(wst3[:, g, 0:128], banks[bt][:, 0:128], AF.Copy),
                 reads=(("ps", bt),), writes=("wst3",))
            P.op("act", lambda e, g=g, bt=bt: e.activation(wst3[:, g, 128:192], banks[bt][:, 0:64], AF.Copy),
                 reads=(("ps", bt),), writes=("wst3",))
        while inq:
            in_fill()
        P.barrier()
        ar.release(m0)
        dump("h0", hres)

        def hkey(sub):
            return ("h", sub)

        def cols(sub):
            return slice(sub * TS, (sub + 1) * TS)

        def rmsnorm_to(sub, gcol, xn, xnkey, sq):
            c = cols(sub)
            P.op("act", lambda e: e.activation(sq, hres[:, :, c], AF.Square), reads=(hkey(sub),), writes=("sq",))
            bk = 7
            mm_group(bk, banks[bk][:, 0:TS], [(onesb, sq[:, k, :], ("sq",)) for k in range(DC)])
            P.op("dve", lambda e: e.tensor_scalar(rstd, banks[bk][:, 0:TS], 1.0 / D, EPS, ALU.mult, ALU.add),
                 reads=(("ps", bk),), writes=("rstd",))
            P.op("act", lambda e: e.activation(rstd, rstd, AF.Sqrt), reads=("rstd",), writes=("rstd",))
            P.op("dve", lambda e: e.reciprocal(rstd, rstd), reads=("rstd",), writes=("rstd",))
            for k in range(DC):
                P.op("dve", lambda e, k=k: e.scalar_tensor_tensor(xn[:, k, :], hres[:, k, c], vecs[:, gcol + k:gcol + k + 1],
                                                                  rstd, ALU.mult, ALU.mult),
                     reads=(hkey(sub), "rstd"), writes=(xnkey,))

        def ffn(st, which, xn, hid, sq, sil, w13, w2s, fill=None):
            gcol = V_F1N if which == 0 else V_F2N
            subs = [st * SPT + i for i in range(SPT)]
            for i, sub in enumerate(subs):
                rmsnorm_to(sub, gcol, xn[i], ("xn", i), sq)
            w2s.prefetch()
            for f in range(FC):
                wa, ka = w13.get()
                wb, kb = w13.get()
                for i, sub in enumerate(subs):
                    b1 = bank([0, 1, 2, 3, 4, 5])
                    b3 = bank([0, 1, 2, 3, 4, 5])
                    mm_group(b1, banks[b1][:, 0:TS], [(wa[:, k, :], xn[i][:, k, :], (ka, ("xn", i))) for k in range(DC)])
                    mm_group(b3, banks[b3][:, 0:TS], [(wb[:, k, :], xn[i][:, k, :], (kb, ("xn", i))) for k in range(DC)])
                    sl = sil[(f * SPT + i) % 2]
                    skey = ("sil", (f * SPT + i) % 2)
                    P.op("act", lambda e, sl=sl, b1=b1: e.activation(sl, banks[b1][:, 0:TS], AF.Silu),
                         reads=(("ps", b1),), writes=(skey,))
                    P.op("dve", lambda e, sl=sl, b3=b3, i=i, f=f: e.tensor_tensor(hid[i][:, f, :], sl, banks[b3][:, 0:TS], ALU.mult),
                         reads=(skey, ("ps", b3)), writes=(("hid", i), "gh") if f == 0 else (("hid", i),))
                    if fill is not None:
                        fill(3)
            for d in range(DC):
                if fill is not None:
                    fill(8)
                wc, kc_ = w2s.get()
                for i, sub in enumerate(subs):
                    bo = bank([0, 1, 2, 3])
                    mm_group(bo, banks[bo][:, 0:TS], [(wc[:, f, :], hid[i][:, f, :], (kc_, ("hid", i), "gh")) for f in range(FC)])
                    c = cols(sub)
                    P.op("dve", lambda e, bo=bo, d=d, c=c: e.scalar_tensor_tensor(hres[:, d, c], banks[bo][:, 0:TS], 0.5,
                                                                                  hres[:, d, c], ALU.mult, ALU.add),
                         reads=(("ps", bo), hkey(sub)), writes=(hkey(sub),))

        def w13_units(which):
            u = []
            for f in range(FC):
                u.append((w1a[which][f], DC, 128))
                u.append((w3a[which][f], DC, 128))
            return u

        def w2_units(which):
            return [(w2a[which][d], FC, 128) for d in range(DC)]

        def wcols_units(w, kc, colchunks):
            return [(w[o], kc, 128) for o in colchunks]

        m1 = ar.mark()
        xn = [ar.bf16(DC, TS) for _ in range(SPT)]
        hid = [ar.bf16(FC, TS) for _ in range(SPT)]
        sq = ar.bf16(DC, TS)
        sil = [ar.f32(TS) for _ in range(2)]
        w13 = WStream(P, ar, "w13", 8, DC, 128, group=2)
        w2s = WStream(P, ar, "w2", 3, FC, 128)
        wk8 = WStream(P, ar, "wk8", 4, DC, 128, group=2)
        for st in range(NST):
            w13.plan(w13_units(0))
            w2s.plan(w2_units(0))
            wk8.plan(wcols_units(w_in, DC, [8, 9, 10, 11]))
            cu = []
            for q in range(4):
                cu += wcols_units(w_in, DC, [q, 4 + q])
            wk8.plan(cu)
        for st in range(NST):
            subs = [st * SPT + i for i in range(SPT)]
            ffn(st, 0, xn, hid, sq, sil, w13, w2s)
            wk8.prefetch()
            for i, sub in enumerate(subs):
                rmsnorm_to(sub, V_MIXN, xn[i], ("xn", i), sq)
            for q in range(4):
                wu, ku = wk8.get()
                for i, sub in enumerate(subs):
                    b = bank([0, 1, 2, 3, 4, 5])
                    mm_group(b, banks[b][:, 0:TS], [(wu[:, k, :], xn[i][:, k, :], (ku, ("xn", i))) for k in range(DC)])
                    P.op("act", lambda e, b=b, q=q, sub=sub: e.activation(ufm[:, q, cols(sub)], banks[b][:, 0:TS], AF.Copy),
                         reads=(("ps", b),), writes=(("ufm", sub),))
            for q in range(4):
                wv_, kv = wk8.get()
                wg_, kg = wk8.get()
                for i, sub in enumerate(subs):
                    bv = bank([0, 1, 2, 3, 4, 5])
                    bg = bank([0, 1, 2, 3, 4, 5])
                    mm_group(bv, banks[bv][:, 0:TS], [(wv_[:, k, :], xn[i][:, k, :], (kv, ("xn", i))) for k in range(DC)])
                    mm_group(bg, banks[bg][:, 0:TS], [(wg_[:, k, :], xn[i][:, k, :], (kg, ("xn", i))) for k in range(DC)])
                    sl = sil[(q * SPT + i) % 2]
                    skey = ("sil", (q * SPT + i) % 2)
                    P.op("act", lambda e, sl=sl, bg=bg: e.activation(sl, banks[bg][:, 0:TS], AF.Sigmoid),
                         reads=(("ps", bg),), writes=(skey,))
                    zc = slice(HALO + sub * TS, HALO + (sub + 1) * TS)
                    P.op("dve", lambda e, sl=sl, bv=bv, q=q, zc=zc: e.tensor_tensor(zall[:, q, zc], sl, banks[bv][:, 0:TS], ALU.mult),
                         reads=(skey, ("ps", bv)), writes=(("z", sub),))
        P.barrier()
        dump("h1", hres)
        dump("u", ufm)
        dump("z", zall)
        ar.release(m1)

        m2 = ar.mark()
        eraw = ar.f32(4096)
        esel = eraw.bitcast(BF16).rearrange("p (a b) -> p a b", a=64)
        Pst = ar.f32(64)
        W2t = ar.f32(64)
        esem = P.dsem("e")
        P.op("pool", lambda e: e.dma_start(out=esel, in_=esel_d.rearrange("p (a b) -> p a b", a=64)),
             writes=("esel",), dsem=esem)
        P.barrier()

        ug = ar.bf16(G, NCH)
        vxs = ar.bf16(G, NCH + 1)
        vy = ar.bf16(G, NCH)
        gz = ar.f32(6 * TS)
        gath = gz[:, 0:2 * 192].rearrange("p (a b) -> p a b", a=2)
        gtmp = [[gz[:, (3 * a + b) * TS:(3 * a + b + 1) * TS] for b in range(3)] for a in range(2)]
        pay = ar.f32(192)
        acc = ar.f32(192)
        NB, KB = 16, 16
        Rall = ar.f32(2, G, NB + 1)
        ucont = vy.rearrange("p g c -> p (g c)").rearrange("p (q j c) -> p q j c", q=4, j=8)
        for q in range(4):
            for j in range(8):
                if (q * 8 + j) % 2 == 0:
                    P.op("act", lambda e, q=q, j=j: e.activation(ucont[:, q, j, :], ufm[:, q, j::8], AF.Copy), writes=("vy",))
                else:
                    P.op("dve", lambda e, q=q, j=j: e.tensor_copy(ucont[:, q, j, :], ufm[:, q, j::8]), writes=("vy",))
        for g in range(G):
            q, gl = g // 8, g % 8
            bk = bank([0, 1, 2, 3, 4, 5])
            terms = []
            for kb in range(8):
                j = 7 - kb
                terms.append((esel[:, gl * 8 + kb, :], ucont[:, q, j, :], ("vy",)))
            mm_group(bk, banks[bk][:, 0:NCH], terms)
            eng = "act" if g % 2 == 0 else "dve"
            if eng == "act":
                P.op("act", lambda e, g=g, bk=bk: e.activation(ug[:, g, :], banks[bk][:, 0:NCH], AF.Copy),
                     reads=(("ps", bk),), writes=(("ug", g),))
            else:
                P.op("dve", lambda e, g=g, bk=bk: e.tensor_copy(ug[:, g, :], banks[bk][:, 0:NCH]),
                     reads=(("ps", bk),), writes=(("ug", g),))
        for g in range(G):
            bx = bank([4, 5, 6, 7])
            by = bank([4, 5, 6, 7])
            mm_group(bx, banks[bx][:, 0:NCH], [(wst3[:, g, 0:128], ug[:, g, :], (("ug", g),))])
            mm_group(by, banks[by][:, 0:NCH], [(wst3[:, g, 64:192], ug[:, g, :], (("ug", g),))])
            P.op("act", lambda e, g=g, bx=bx: e.activation(vxs[:, g, 1:NCH + 1], banks[bx][:, 0:NCH], AF.Copy),
                 reads=(("ps", bx),), writes=("vxs",))
            P.op("dve", lambda e, g=g, by=by: e.tensor_copy(vy[:, g, :], banks[by][:, 0:NCH]),
                 reads=(("ps", by),), writes=("vy",))
        P.barrier()
        Lst = eraw[:, 0:1024]
        W2b = eraw[:, 1024:2048]
        CAr = eraw[:, 2048:3072]
        CBr = eraw[:, 3072:4096]
        pstr = list(Pst.ap[0])

        def v4(a):
            return a.rearrange("p (a g b) -> p a g b", a=2, g=G)

        def v3(a):
            return a.rearrange("p (a n) -> p a n", a=2)

        def swp(a):
            return bass.AP(a.tensor, a.offset + 512, [list(a.ap[0]), [-512, 2], [1, 512]])

        S = lambda fn, r=("scan",), w=("scan",), eng="dve": P.op(eng, fn, reads=r, writes=w)
        S(lambda e: e.tensor_copy(CAr.rearrange("p (n b) -> p n b", b=NB), CA.unsqueeze(2).broadcast_to([128, 64, NB])))
        S(lambda e: e.tensor_copy(CBr.rearrange("p (n b) -> p n b", b=NB), CB.unsqueeze(2).broadcast_to([128, 64, NB])))
        S(lambda e: e.memset(Lst, 0.0))
        for j in range(KB):
            S(lambda e: e.tensor_tensor(v3(W2b), v3(CBr), swp(Lst), ALU.mult))
            S(lambda e: e.tensor_tensor(Lst, Lst, CAr, ALU.mult), r=("scan", "lwb"))
            S(lambda e: e.tensor_tensor(Lst, Lst, W2b, ALU.add))
            S(lambda e, j=j: e.tensor_tensor(v4(Lst)[:, 0], v4(Lst)[:, 0], vxs[:, :, 1 + j:1 + j + (NB - 1) * KB + 1:KB], ALU.add),
              r=("scan", "vxs", "vy"))
            S(lambda e, j=j: e.tensor_tensor(v4(Lst)[:, 1], v4(Lst)[:, 1], vy[:, :, j:j + (NB - 1) * KB + 1:KB], ALU.add))
            P.op("act", lambda e, j=j: e.activation(vxs[:, :, 1 + j:1 + j + (NB - 1) * KB + 1:KB], v4(Lst)[:, 0], AF.Copy),
                 reads=("scan",), writes=("lwb", "vxs"))
        Rv = Rall
        Pv = Pst.rearrange("p (a b) -> p a b", a=2)
        W2v = W2t.rearrange("p (a b) -> p a b", a=2)
        CBv = CB.rearrange("p (a b) -> p a b", a=2)
        CAv = CA.rearrange("p (a b) -> p a b", a=2)
        CB16v = CB16.rearrange("p (a b) -> p a b", a=2)
        CA16v = CA16.rearrange("p (a b) -> p a b", a=2)
        RSTR = G * (NB + 1)

        def rsw(b):
            r0 = Rall[:, 0, :, b]
            return bass.AP(r0.tensor, r0.offset + RSTR, [list(r0.ap[0]), [-RSTR, 2], [NB + 1, G]])

        def block_scan(init):
            if init is None:
                S(lambda e: e.memset(Rall[:, :, :, 0], 0.0))
            else:
                S(lambda e: e.tensor_copy(Rall[:, :, :, 0], init.rearrange("p (a b) -> p a b", a=2)), r=("scan", "acc"))
            for b in range(NB):
                S(lambda e, b=b: e.tensor_tensor(W2v, CB16v, rsw(b), ALU.mult))
                S(lambda e, b=b: e.tensor_tensor(Pv, CA16v, Rall[:, :, :, b], ALU.mult))
                S(lambda e: e.tensor_tensor(Pst, Pst, W2t, ALU.add))
                S(lambda e, b=b: e.tensor_tensor(Rall[:, :, :, b + 1], Pv, v4(Lst)[:, :, :, b], ALU.add))

        block_scan(None)
        S(lambda e: e.tensor_tensor(W2v, CBv, rsw(NB), ALU.mult))
        S(lambda e: e.tensor_tensor(Pv, CAv, Rall[:, :, :, NB], ALU.mult))
        S(lambda e: e.tensor_tensor(Pst, Pst, W2t, ALU.add))
        S(lambda e: e.tensor_tensor(Pst[:, 0:32], Pst[:, 0:32], vxs[:, :, NCH], ALU.add), r=("scan", "vxs"))
        S(lambda e: e.tensor_tensor(Pst[:, 32:64], Pst[:, 32:64], vy[:, :, NCH - 1], ALU.add), r=("scan", "vy"))
        P.op("dve", lambda e: e.tensor_copy(pay[:, 0:64], Pst), reads=("scan",), writes=("pay",))
        P.op("dve", lambda e: e.tensor_copy(pay[:, 64:192].rearrange("p (a b) -> p a b", a=4), zall[:, :, NT:NT + HALO]),
             reads=(("z", NSUB - 1),), writes=("pay",))
        xs = P.dsem("xc")
        P.op("pool", lambda e: e.dma_start(out=cc_in[:, :], in_=pay), reads=("pay",), writes=("ccin",), dsem=xs)
        P.op("pool", lambda e: e.collective_compute("AllGather", ALU.bypass, replica_groups=[[2 * b, 2 * b + 1] for b in range(NCORES // 2)],
                                                    ins=[cc_in.ap().opt()], outs=[cc_out.ap().opt()]),
             reads=("ccin",), writes=("ccout",))
        P.op("pool", lambda e: e.dma_start(out=gath, in_=cc_out.ap().rearrange("(r p) n -> p r n", p=128)),
             reads=("ccout",), writes=("gath",), dsem=xs)
        P.op("dve", lambda e: e.tensor_scalar(acc, gath[:, 0, :], onehot[:, 0:1], None, ALU.mult), reads=("gath",), writes=("acc",))
        for r in range(1, 2):
            P.op("dve", lambda e, r=r: e.scalar_tensor_tensor(acc, gath[:, r, :], onehot[:, r:r + 1], acc, ALU.mult, ALU.add),
                 reads=("gath", "acc"), writes=("acc",))
        P.op("dve", lambda e: e.tensor_copy(zall[:, :, 0:HALO], acc[:, 64:192].rearrange("p (a b) -> p a b", a=4)),
             reads=("acc",), writes=("zhalo",))
        P.barrier()
        block_scan(acc[:, 0:64])
        S(lambda e: e.tensor_copy(v4(W2b)[:, 0], Rall[:, 0, :, 0:NB]))
        S(lambda e: e.tensor_copy(v4(W2b)[:, 1], Rall[:, 1, :, 0:NB]))
        S(lambda e: e.tensor_copy(vxs[:, :, 0], Rall[:, 0, :, 0]), r=("scan", "vxs"), w=("scan", "vxs"))
        for j in range(KB):
            S(lambda e: e.tensor_tensor(v3(Lst), v3(CBr), swp(W2b), ALU.mult))
            S(lambda e: e.tensor_tensor(W2b, W2b, CAr, ALU.mult))
            S(lambda e: e.tensor_tensor(W2b, W2b, Lst, ALU.add))
            S(lambda e, j=j: e.tensor_tensor(vxs[:, :, 1 + j:1 + j + (NB - 1) * KB + 1:KB], vxs[:, :, 1 + j:1 + j + (NB - 1) * KB + 1:KB],
                                             v4(W2b)[:, 0], ALU.add), r=("scan", "vxs"), w=("scan", "vxs"))
        P.barrier()
        P.op("pool", lambda e: e.dma_start(out=esel, in_=esel_d.rearrange("p (a b) -> p a b", a=64)),
             writes=("esel",), dsem=esem)
        for g in range(G):
            bk = bank([0, 1, 2, 3, 4, 5])
            mm_group(bk, banks[bk][:, 0:NCH], [(mbf[:, g, :], ug[:, g, :], (("ug", g), "mbf")),
                                               (woutb[:, g, :], vxs[:, g, 0:NCH], ("vxs",))])
            P.op("act", lambda e, g=g, bk=bk: e.activation(ug[:, g, :], banks[bk][:, 0:NCH], AF.Copy),
                 reads=(("ps", bk),), writes=(("ug", g),))
        for q in range(4):
            for t in range(8):
                bk = bank([4, 5, 6, 7])
                mm_group(bk, banks[bk][:, 0:NCH],
                         [(esel[:, t * 8 + gl, :], ug[:, q * 8 + gl, :], (("ug", q * 8 + gl), "esel")) for gl in range(8)])
                ps = banks[bk][:, 0:NCH]
                par = (q * 8 + t) % 2
                a0, a1, a2 = gtmp[par]
                k0, k1, k2 = ("g0", par), ("g1", par), ("g2", par)
                P.op("act", lambda e, ps=ps, a0=a0: e.activation(a0, ps, AF.Square, scale=math.sqrt(0.044715)),
                     reads=(("ps", bk),), writes=(k0,))
                P.op("dve", lambda e, ps=ps, a0=a0, a1=a1: e.scalar_tensor_tensor(a1, a0, 1.0, ps, ALU.add, ALU.mult),
                     reads=(k0, ("ps", bk)), writes=(k1,))
                P.op("act", lambda e, a1=a1, a2=a2: e.activation(a2, a1, AF.Sigmoid, scale=2.0 * math.sqrt(2.0 / math.pi)),
                     reads=(k1,), writes=(k2,))
                P.op("dve", lambda e, ps=ps, q=q, t=t, a2=a2: e.tensor_tensor(ufm[:, q, t::8], a2, ps, ALU.mult),
                     reads=(k2, ("ps", bk)), writes=("yg",))
        P.barrier()
        dump("yg", ufm)
        dump("mbf", mbf)
        dump("wst3", wst3)
        dump("woutb", woutb)
        dump("ug", ug)
        dump("vxs", vxs)
        ar.release(m2)

        ar.size = ARENA
        m5 = ar.mark()
        xn = [ar.bf16(DC, TS) for _ in range(SPT)]
        sq = ar.bf16(DC, TS)
        sil = [ar.f32(TS) for _ in range(2)]
        cacc = [ar.f32(4, TS) for _ in range(SPT)]
        cs = [ar.bf16(4, TS) for _ in range(SPT)]
        cab = ar.bf16(4, TS)
        csq = ar.bf16(4, TS)
        m4 = ar.mark()
        gates = [ar.bf16(16, TS) for _ in range(SPT)]
        mA = ar.mark()
        ar.release(m4)
        hid = [ar.bf16(FC, TS) for _ in range(SPT)]
        ar.off = max(ar.off, mA)
        mrg = [ar.bf16(DC, TS) for _ in range(SPT)]
        ysb = [ar.f32(TS) for _ in range(2)]
        mt = [ar.f32(TS) for _ in range(2)]
        lnm = ar.f32(TS)
        lnv = ar.f32(TS)
        w13 = WStream(P, ar, "v13", 8, DC, 128, group=2)
        w2s = WStream(P, ar, "v2", 2, FC, 128)
        wk8 = WStream(P, ar, "vk8", 4, DC, 128)
        wk4 = WStream(P, ar, "vk4", 6, 4, 128, group=3)
        for st in range(NST):
            wk8.plan(wcols_units(w_in, DC, list(range(12, 28))))
            u4 = []
            for d in range(DC):
                u4 += wcols_units(conv_proj, 4, [d]) + wcols_units(w_v, 4, [d]) + wcols_units(w_g, 4, [d])
            wk4.plan(u4)
            wk8.plan(wcols_units(w_out, DC, list(range(DC))))
            w13.plan(w13_units(1))
            w2s.plan(w2_units(1))
        def conv_thunks(st):
            subs = [st * SPT + i for i in range(SPT)]
            T = []
            for k in range(31):
                for i, sub in enumerate(subs):
                    for q in range(4):
                        zc = slice(sub * TS + 2 + k, sub * TS + 2 + k + TS)
                        wcol = vecs[:, V_DW + q * 31 + k:V_DW + q * 31 + k + 1]
                        if k == 0:
                            T.append(lambda i=i, q=q, zc=zc, wcol=wcol, sub=sub: P.op("dve", lambda e: e.tensor_scalar(
                                cacc[i][:, q, :], zall[:, q, zc], wcol, vecs[:, V_DWB + q:V_DWB + q + 1], ALU.mult, ALU.add),
                                reads=(("z", sub), ("z", sub - 1), "zhalo", ("cs", i)), writes=(("cacc", i, q),)))
                        else:
                            T.append(lambda i=i, q=q, zc=zc, wcol=wcol: P.op("dve", lambda e: e.scalar_tensor_tensor(
                                cacc[i][:, q, :], zall[:, q, zc], wcol, cacc[i][:, q, :], ALU.mult, ALU.add),
                                reads=(("cacc", i, q),), writes=(("cacc", i, q),)))
            for i, sub in enumerate(subs):
                ck = tuple(("cacc", i, q) for q in range(4))
                T.append(lambda i=i, ck=ck: P.op("act", lambda e: e.activation(cab, cacc[i], AF.Copy), reads=ck, writes=("cab",)))
                T.append(lambda i=i, ck=ck: P.op("act", lambda e: e.activation(csq, cacc[i], AF.Square), reads=ck, writes=("csq",)))
                T.append(lambda: mm_group(6, banks[6][:, 0:TS], [(onesb, cab[:, q, :], ("cab",)) for q in range(4)]))
                T.append(lambda: P.op("dve", lambda e: e.tensor_scalar(lnm, banks[6][:, 0:TS], 1.0 / 512, None, ALU.mult),
                                      reads=(("ps", 6),), writes=("lnm",)))
                T.append(lambda: mm_group(6, banks[6][:, 0:TS], [(onesb, csq[:, q, :], ("csq",)) for q in range(4)]))
                T.append(lambda: P.op("dve", lambda e: e.tensor_tensor(lnv, lnm, lnm, ALU.mult), reads=("lnm",), writes=("lnv",)))
                T.append(lambda: P.op("dve", lambda e: e.scalar_tensor_tensor(lnv, banks[6][:, 0:TS], 1.0 / 512, lnv, ALU.mult, ALU.subtract),
                                      reads=(("ps", 6), "lnv"), writes=("lnv",)))
                T.append(lambda: P.op("dve", lambda e: e.tensor_scalar(lnv, lnv, EPS, None, ALU.add), reads=("lnv",), writes=("lnv",)))
                T.append(lambda: P.op("act", lambda e: e.activation(lnv, lnv, AF.Sqrt), reads=("lnv",), writes=("lnv",)))
                T.append(lambda: P.op("dve", lambda e: e.reciprocal(lnv, lnv), reads=("lnv",), writes=("lnv",)))
                for q in range(4):
                    T.append(lambda i=i, q=q: P.op("dve", lambda e: e.tensor_tensor(cacc[i][:, q, :], cacc[i][:, q, :], lnm, ALU.subtract),
                                                   reads=(("cacc", i, q), "lnm"), writes=(("cacc", i, q),)))
                    T.append(lambda i=i, q=q: P.op("dve", lambda e: e.tensor_tensor(cacc[i][:, q, :], cacc[i][:, q, :], lnv, ALU.mult),
                                                   reads=(("cacc", i, q), "lnv"), writes=(("cacc", i, q),)))
                    T.append(lambda i=i, q=q: P.op("act", lambda e: e.activation(cs[i][:, q, :], cacc[i][:, q, :], AF.Silu,
                                                                                 bias=vecs[:, V_LNB + q:V_LNB + q + 1],
                                                                                 scale=vecs[:, V_LNG + q:V_LNG + q + 1]),
                                                   reads=(("cacc", i, q),), writes=(("cs", i),)))
            return T

        cur = conv_thunks(0)
        nxt = []

        def fill(n):
            for _ in range(n):
                if cur:
                    cur.pop(0)()
                elif nxt:
                    nxt.pop(0)()
                else:
                    return

        def drain_cur():
            while cur:
                cur.pop(0)()

        for st in range(NST):
            subs = [st * SPT + i for i in range(SPT)]
            if st + 1 < NST:
                nxt.extend(conv_thunks(st + 1))
            wk8.prefetch()
            wk4.prefetch()
            for i, sub in enumerate(subs):
                rmsnorm_to(sub, V_MIXN, xn[i], ("xn", i), sq)
            for o in range(16):
                wu, ku = wk8.get()
                for i, sub in enumerate(subs):
                    b = bank([0, 1, 2, 3, 4, 5])
                    mm_group(b, banks[b][:, 0:TS], [(wu[:, k, :], xn[i][:, k, :], (ku, ("xn", i))) for k in range(DC)])
                    P.op("act", lambda e, b=b, o=o, i=i: e.activation(gates[i][:, o, :], banks[b][:, 0:TS], AF.Sigmoid,
                                                                     bias=vecs[:, V_BG + o:V_BG + o + 1]),
                         reads=(("ps", b),), writes=(("gates", i), "gh") if o == 0 else (("gates", i),))
                    fill(1)
            drain_cur()
            for d in range(DC):
                wc_, kc_ = wk4.get()
                wv_, kv = wk4.get()
                wg_, kg = wk4.get()
                for i, sub in enumerate(subs):
                    c = cols(sub)
                    bc_ = bank([0, 1, 2, 3, 4, 5])
                    bv = bank([0, 1, 2, 3, 4, 5])
                    bg = bank([0, 1, 2, 3, 4, 5])
                    mm_group(bc_, banks[bc_][:, 0:TS], [(wc_[:, k, :], cs[i][:, k, :], (kc_, ("cs", i))) for k in range(4)])
                    mm_group(bv, banks[bv][:, 0:TS], [(wv_[:, k, :], ufm[:, k, c], (kv, "yg")) for k in range(4)])
                    mm_group(bg, banks[bg][:, 0:TS], [(wg_[:, k, :], ufm[:, k, c], (kg, "yg")) for k in range(4)])
                    j = (d * SPT + i) % 2
                    P.op("act", lambda e, j=j, bg=bg: e.activation(ysb[j], banks[bg][:, 0:TS], AF.Sigmoid),
                         reads=(("ps", bg),), writes=(("ysb", j),))
                    P.op("dve", lambda e, j=j, bv=bv: e.tensor_tensor(ysb[j], ysb[j], banks[bv][:, 0:TS], ALU.mult),
                         reads=(("ysb", j), ("ps", bv)), writes=(("ysb", j),))
                    P.op("dve", lambda e, j=j, i=i, d=d, bc_=bc_: e.tensor_tensor(mt[j], gates[i][:, d, :], banks[bc_][:, 0:TS], ALU.mult),
                         reads=(("gates", i), "gh", ("ps", bc_)), writes=(("mt", j),))
                    P.op("dve", lambda e, j=j, i=i, d=d: e.tensor_tensor(ysb[j], ysb[j], gates[i][:, 8 + d, :], ALU.mult),
                         reads=(("ysb", j), ("gates", i), "gh"), writes=(("ysb", j),))
                    P.op("dve", lambda e, j=j, i=i, d=d: e.tensor_tensor(mrg[i][:, d, :], mt[j], ysb[j], ALU.add),
                         reads=(("mt", j), ("ysb", j)), writes=(("mrg", i),))
            w13.prefetch()
            for d in range(DC):
                wo, ko = wk8.get()
                for i, sub in enumerate(subs):
                    c = cols(sub)
                    bo = bank([0, 1, 2, 3])
                    mm_group(bo, banks[bo][:, 0:TS], [(wo[:, k, :], mrg[i][:, k, :], (ko, ("mrg", i))) for k in range(DC)])
                    P.op("dve", lambda e, bo=bo, d=d, c=c: e.tensor_tensor(hres[:, d, c], hres[:, d, c], banks[bo][:, 0:TS], ALU.add),
                         reads=(("ps", bo), hkey(sub)), writes=(hkey(sub),))
                    fill(1)
            ffn(st, 1, xn, hid, sq, sil, w13, w2s, fill=fill)
            cur.extend(nxt)
            del nxt[:]
        P.barrier()
        ar.release(m5)

        fsq = [ar.bf16(DC, 128) for _ in range(2)]
        frs = [ar.f32(128) for _ in range(2)]
        fon = [ar.f32(DC, 128) for _ in range(2)]
        oblk = [ar.f32(D) for _ in range(2)]
        osem = [P.dsem("o") for _ in range(2)]

        def fin_a(bi):
            r0 = bi * 128
            nr = min(128, NT - r0)
            s = bi % 2
            c = slice(r0, r0 + nr)
            sb = 4 + s
            P.op("act", lambda e: e.activation(fsq[s][:, :, 0:nr], hres[:, :, c], AF.Square), writes=(("fsq", s),))
            mm_group(sb, banks[sb][:, 0:nr], [(onesb, fsq[s][:, k, 0:nr], (("fsq", s),)) for k in range(DC)])
            P.op("dve", lambda e: e.tensor_scalar(frs[s][:, 0:nr], banks[sb][:, 0:nr], 1.0 / D, EPS, ALU.mult, ALU.add),
                 reads=(("ps", sb),), writes=(("frs", s),))
            P.op("act", lambda e: e.activation(frs[s][:, 0:nr], frs[s][:, 0:nr], AF.Sqrt), reads=(("frs", s),), writes=(("frs", s),))
            P.op("dve", lambda e: e.reciprocal(frs[s][:, 0:nr], frs[s][:, 0:nr]), reads=(("frs", s),), writes=(("frs", s),))
            for k in range(DC):
                P.op("dve", lambda e, k=k: e.scalar_tensor_tensor(
                    fon[s][:, k, 0:nr], hres[:, k, c], vecs[:, V_FIN + k:V_FIN + k + 1], frs[s][:, 0:nr], ALU.mult, ALU.mult),
                    reads=(("frs", s),), writes=(("fon", s),))

        def fin_b(bi):
            r0 = bi * 128
            nr = min(128, NT - r0)
            s = bi % 2
            for hb in range(2):
                bk = 2 * s + hb
                for kk in range(4):
                    k = hb * 4 + kk
                    P.op("pe", lambda e, bk=bk, kk=kk, k=k: e.transpose(banks[bk][0:nr, kk * 128:(kk + 1) * 128], fon[s][:, k, 0:nr], ident),
                         reads=(("fon", s),), writes=(("ps", bk),) if kk in (0, 3) else (), signal=(kk == 3))
                P.op("act", lambda e, bk=bk, hb=hb: e.activation(oblk[s][0:nr, hb * 512:(hb + 1) * 512], banks[bk][0:nr, :], AF.Copy),
                     reads=(("ps", bk),), writes=(("oblk", s),))
            P.op("sp", lambda e: e.dma_start(out=out_d[r0:r0 + nr, :], in_=oblk[s][0:nr, :]),
                 reads=(("oblk", s),), dsem=osem[s])

        fin_a(0)
        for bi in range(nblk):
            if bi + 1 < nblk:
                fin_a(bi + 1)
            fin_b(bi)
        P.barrier()
        P.emit()
    return nc


NVEC = 60 + 124
NSP = 136 + 2048

_CACHE = {}
_DBG = False
_DBG_HOOK = None


def _consts():
    ident = np.eye(128, dtype=np.float32)
    r = np.arange(128)
    kb, t = r[:, None] // 16, r[None, :] // 16
    mask = ((kb + t) >= 7).astype(np.float32)
    adi = (((kb + t) == 7) & ((r[:, None] % 16) == (r[None, :] % 16))).astype(np.float32)
    cst = np.concatenate([ident, mask, adi, np.zeros((128, 128), np.float32)], axis=1)
    esel = np.zeros((128, 64, 128), np.float32)
    for a in range(8):
        for b in range(8):
            for h in range(16):
                esel[a * 16 + h, a * 8 + b, b * 16 + h] = 1.0
    return np.ascontiguousarray(cst), np.ascontiguousarray(esel.reshape(128, 64 * 128))


def kernel(x, meta_tokens, ffn1_norm, ffn1_w1, ffn1_w3, ffn1_w2, mix_norm, w_in, b_gate,
           conv_dw, conv_dw_b, conv_ln_g, conv_ln_b, conv_proj,
           ssm_lam_re, ssm_lam_im, ssm_log_dt, ssm_b_re, ssm_b_im, ssm_c_re, ssm_c_im,
           ssm_d, ssm_w_v, ssm_w_g, w_out, ffn2_norm, ffn2_w1, ffn2_w3, ffn2_w2, final_norm):
    f = lambda a: np.ascontiguousarray(np.asarray(a, dtype=np.float32))
    x = f(x)
    B = x.shape[0]
    def colv(v, nchunk):
        return f(v).reshape(nchunk, 128).T
    vecs = np.concatenate([
        colv(ffn1_norm[0], 8), colv(mix_norm[0], 8), colv(ffn2_norm[0], 8), colv(final_norm, 8),
        colv(b_gate[0], 16), colv(conv_dw_b[0], 4), colv(conv_ln_g[0], 4), colv(conv_ln_b[0], 4),
        f(conv_dw[0]).T.reshape(4, 128, 31).transpose(1, 0, 2).reshape(128, 124),
    ], axis=1)
    assert vecs.shape == (128, NVEC)
    dup = lambda a: np.concatenate([a, a], axis=0)
    lre = dup(f(ssm_lam_re[0]).T)
    lim = dup(f(ssm_lam_im[0]).T)
    ldt = np.broadcast_to(f(ssm_log_dt[0])[None, :], (128, G))
    dcol = np.tile(f(ssm_d[0]).reshape(G, 16).T, (8, 1))
    bre = dup(f(ssm_b_re[0]).transpose(1, 0, 2).reshape(64, G * 16))
    bim = dup(f(ssm_b_im[0]).transpose(1, 0, 2).reshape(64, G * 16))
    cre = dup(f(ssm_c_re[0]).transpose(2, 0, 1).reshape(64, G * 16))
    cim = dup(f(ssm_c_im[0]).transpose(2, 0, 1).reshape(64, G * 16))
    cst, esel = _consts()

    def tile_w(w):
        w = f(w)
        K, N = w.shape
        return np.ascontiguousarray(w.reshape(K // 128, 128, N // 128, 128).transpose(2, 1, 0, 3).reshape(N // 128, 128, K))
    wt = {"ffn1_w1": tile_w(ffn1_w1[0]), "ffn1_w3": tile_w(ffn1_w3[0]), "ffn1_w2": tile_w(ffn1_w2[0]),
          "ffn2_w1": tile_w(ffn2_w1[0]), "ffn2_w3": tile_w(ffn2_w3[0]), "ffn2_w2": tile_w(ffn2_w2[0]),
          "w_in": tile_w(w_in[0]), "conv_proj": tile_w(conv_proj[0]), "ssm_w_v": tile_w(ssm_w_v[0]),
          "ssm_w_g": tile_w(ssm_w_g[0]), "w_out": tile_w(w_out[0])}
    meta = f(meta_tokens)
    half = NT - 16
    in_maps = []
    for core in range(NCORES):
        b, s = core // 2, core % 2
        if s == 0:
            xin = np.concatenate([meta, x[b, 0:half]], axis=0)
        else:
            xin = x[b, half:]
        oh = np.zeros((128, 8), np.float32)
        if s == 1:
            oh[:, 0] = 1.0
        sp = np.concatenate([lre, lim, ldt, dcol, oh, bre, bim, cre, cim], axis=1)
        assert sp.shape == (128, NSP)
        in_maps.append({
            "xin": f(xin), "vecs": f(vecs), "sp": f(sp), "cst": cst, "esel": esel, **wt,
        })
    if "nc" not in _CACHE:
        nc = bass.Bass("TRN2", target_bir_lowering=False)
        _CACHE["nc"] = build(nc)
    res = run_bass_kernel_spmd(_CACHE["nc"], in_maps, core_ids=list(range(NCORES)))
    if _DBG_HOOK is not None:
        _DBG_HOOK(res)
    out = np.empty((B, 4096, D), np.float32)
    for core in range(NCORES):
        b, s = core // 2, core % 2
        o = np.asarray(res.results[core]["out"], dtype=np.float32)
        if s == 0:
            out[b, 0:half] = o[16:]
        else:
            out[b, half:] = o
    return out
```
